# Optimizing a Trainium2 kernel written in Bass

```python
import math
import jax, jax.numpy as jnp
from jax import lax
import numpy as np

D_MODEL = 2048
BATCH = 4
SEQ = 4096
DEPTH = 4

CTX_LEN = 256
GRID_W = 64
N_MIXERS = 3
RMS_EPS = 1e-6
F32 = jnp.float32

DN_DK = 128
DN_DV = 128
DN_HK = D_MODEL // 128
DN_HV = 2 * DN_HK
DN_QK = DN_HK * DN_DK
DN_VW = DN_HV * DN_DV
DN_CONV_CH = 2 * DN_QK + DN_VW
DN_CONV_K = 5
DN_CHUNK = 64
DN_PROJ = DN_CONV_CH + DN_VW + 4 * DN_HV

DA_DK = 128
DA_H = D_MODEL // (2 * DA_DK)
DA_QK = 2 * DA_H * DA_DK
DA_VW = DA_H * 2 * DA_DK
DA_PROJ = 2 * DA_QK + 2 * DA_VW
DA_BLOCK = 128
ROPE_THETA = 10000.0

NA_DH = 128
NA_H = D_MODEL // NA_DH
NA_W = NA_H * NA_DH
NA_PROJ = 4 * NA_W
NA_WR = 8
NA_WC = 16

kernel_name = 'hybrid_dit_deltanet_diffattn_natten'


def rmsnorm(x, w, eps=RMS_EPS):
    xf = x.astype(F32)
    y = xf * lax.rsqrt(jnp.mean(xf * xf, axis=-1, keepdims=True) + eps)
    return (y * w.astype(F32)).astype(x.dtype)


def l2norm(x):
    xf = x.astype(F32)
    return xf * lax.rsqrt(jnp.sum(xf * xf, axis=-1, keepdims=True) + 1e-6)


def rope_2d_tables(n, head_dim):
    t = jnp.arange(n)
    row = (t // GRID_W).astype(F32)
    col = (t % GRID_W).astype(F32)
    half = head_dim // 2
    inv = ROPE_THETA ** (-jnp.arange(0, half, 2, dtype=F32) / half)
    ang_r = row[:, None] * inv
    ang_c = col[:, None] * inv
    ang = jnp.concatenate([ang_r, ang_r, ang_c, ang_c], axis=-1)
    return jnp.cos(ang), jnp.sin(ang)


def apply_rope_2d(x, cos, sin):
    a1, a2, b1, b2 = jnp.split(x, 4, axis=-1)
    rot = jnp.concatenate([-a2, a1, -b2, b1], axis=-1)
    return (x * cos[:, None, :] + rot * sin[:, None, :]).astype(x.dtype)


def short_conv_silu(x, w):
    k = jnp.transpose(w)[:, None, :].astype(x.dtype)
    y = lax.conv_general_dilated(x, k, window_strides=(1,),
                                 padding=[(DN_CONV_K // 2, DN_CONV_K // 2)],
                                 dimension_numbers=('NWC', 'WIO', 'NWC'),
                                 feature_group_count=x.shape[-1])
    return jax.nn.silu(y)


def softmax_attend(q, k, v):
    s = jnp.einsum('bqhd,bkhd->bhqk', q, k, preferred_element_type=F32)
    p = jax.nn.softmax(s, axis=-1)
    return jnp.einsum('bhqk,bkhe->bqhe', p.astype(v.dtype), v, preferred_element_type=F32)


def chunk_gated_delta(q, k, v, g, beta, s0):
    B, n, H, dk = k.shape
    dv = v.shape[-1]
    C = DN_CHUNK
    nc = n // C

    def chunks(t):
        t = t.astype(F32).reshape((B, nc, C, H) + t.shape[3:])
        return jnp.moveaxis(t, (1, 3), (0, 2))

    qc = chunks(q) * (dk ** -0.5)
    kc = chunks(k)
    vc = chunks(v)
    bc = chunks(beta)
    gc = jnp.cumsum(chunks(g), axis=-1)
    idx = jnp.arange(C)
    incl = idx[:, None] >= idx[None, :]
    strict = idx[:, None] > idx[None, :]
    diff = gc[..., :, None] - gc[..., None, :]
    decay = jnp.where(incl, jnp.exp(jnp.where(incl, diff, 0.0)), 0.0)
    kb = kc * bc[..., None]
    lower = jnp.where(strict, jnp.einsum('nbhid,nbhjd->nbhij', kb, kc) * decay, 0.0)
    eye = jnp.eye(C, dtype=F32)
    tinv = lax.linalg.triangular_solve(eye + lower, jnp.broadcast_to(eye, lower.shape),
                                       left_side=True, lower=True, unit_diagonal=True)
    u = tinv @ (vc * bc[..., None])
    w = tinv @ (kb * jnp.exp(gc)[..., None])
    a_intra = jnp.einsum('nbhid,nbhjd->nbhij', qc, kc) * decay
    q_dec = qc * jnp.exp(gc)[..., None]
    k_dec = kc * jnp.exp(gc[..., -1:] - gc)[..., None]
    g_last = jnp.exp(gc[..., -1])

    def step(s, inp):
        q_i, k_i, u_i, w_i, a_i, gl_i = inp
        v_new = u_i - w_i @ s
        o_i = q_i @ s + a_i @ v_new
        s = s * gl_i[..., None, None] + jnp.einsum('bhck,bhcv->bhkv', k_i, v_new)
        return s, o_i

    s_final, o = lax.scan(step, s0.astype(F32), (q_dec, k_dec, u, w, a_intra, g_last))
    o = jnp.moveaxis(o, (0, 2), (1, 3)).reshape(B, n, H, dv)
    return o, s_final


def deltanet_mixer(h_lat, h_ctx, w_in, conv_w, a_log, dt_bias, onorm_w, w_out, need_ctx):
    B = h_lat.shape[0]
    rep = DN_HV // DN_HK

    def project(h):
        n = h.shape[1]
        qkv, z, ba = jnp.split(h @ w_in, [DN_CONV_CH, DN_CONV_CH + DN_VW], axis=-1)
        q, k, v = jnp.split(short_conv_silu(qkv, conv_w), [DN_QK, 2 * DN_QK], axis=-1)
        q = jnp.repeat(l2norm(q.reshape(B, n, DN_HK, DN_DK)), rep, axis=2)
        k = jnp.repeat(l2norm(k.reshape(B, n, DN_HK, DN_DK)), rep, axis=2)
        v = v.reshape(B, n, DN_HV, DN_DV)
        ba = ba.reshape(B, n, 2, 2, DN_HV).astype(F32)
        beta = jax.nn.sigmoid(ba[:, :, :, 0])
        g = -jnp.exp(a_log.astype(F32)) * jax.nn.softplus(ba[:, :, :, 1] + dt_bias.astype(F32))
        return q, k, v, z, g, beta

    ql, kl, vl, zl, gl, bl = project(h_lat)
    qc, kc, vc, zc, gc, bc = project(h_ctx)
    outs_lat = []
    outs_ctx = []
    for d in range(2):
        rev = (lambda t: jnp.flip(t, axis=1)) if d == 1 else (lambda t: t)
        s0 = jnp.zeros((B, DN_HV, DN_DK, DN_DV), F32)
        oc, s_ctx = chunk_gated_delta(rev(qc), rev(kc), rev(vc), rev(gc[:, :, d]), rev(bc[:, :, d]), s0)
        ol, _ = chunk_gated_delta(rev(ql), rev(kl), rev(vl), rev(gl[:, :, d]), rev(bl[:, :, d]), s_ctx)
        outs_lat.append(rev(ol))
        if need_ctx:
            outs_ctx.append(rev(oc))

    def finish(o, z):
        n = o.shape[1]
        zh = z.reshape(B, n, DN_HV, DN_DV).astype(F32)
        y = rmsnorm(o, onorm_w) * jax.nn.silu(zh)
        return y.reshape(B, n, DN_VW).astype(z.dtype) @ w_out

    out_lat = finish(outs_lat[0] + outs_lat[1], zl)
    out_ctx = finish(outs_ctx[0] + outs_ctx[1], zc) if need_ctx else None
    return out_lat, out_ctx


def diff_attn_mixer(h_lat, h_ctx, w_in, lam, subln_w, w_out, layer_idx, need_ctx):
    B, N, _ = h_lat.shape
    lambda_init = 0.8 - 0.6 * math.exp(-0.3 * layer_idx)
    lam = lam.astype(F32)
    lam_full = jnp.exp(jnp.sum(lam[0] * lam[1])) - jnp.exp(jnp.sum(lam[2] * lam[3])) + lambda_init
    cos, sin = rope_2d_tables(N, DA_DK)

    def project(h):
        n = h.shape[1]
        q, k, v, gate = jnp.split(h @ w_in, [DA_QK, 2 * DA_QK, 2 * DA_QK + DA_VW], axis=-1)
        return (q.reshape(B, n, 2 * DA_H, DA_DK), k.reshape(B, n, 2 * DA_H, DA_DK),
                v.reshape(B, n, DA_H, 2 * DA_DK), gate)

    def attend(q, k, v):
        s = jnp.einsum('bqhjd,bkhjd->bhjqk', q, k, preferred_element_type=F32) * (DA_DK ** -0.5)
        p = jax.nn.softmax(s, axis=-1)
        a = p[:, :, 0] - lam_full * p[:, :, 1]
        return jnp.einsum('bhqk,bkhe->bqhe', a.astype(v.dtype), v, preferred_element_type=F32)

    ql, kl, vl, gl = project(h_lat)
    qc, kc, vc, gc = project(h_ctx)
    ql = apply_rope_2d(ql, cos, sin).reshape(B, N, DA_H, 2, DA_DK)
    kl = apply_rope_2d(kl, cos, sin).reshape(B, N, DA_H, 2, DA_DK)
    qc = qc.reshape(B, -1, DA_H, 2, DA_DK)
    kc = kc.reshape(B, -1, DA_H, 2, DA_DK)
    k_all = jnp.concatenate([kc, kl], axis=1)
    v_all = jnp.concatenate([vc, vl], axis=1)
    q_blocks = jnp.moveaxis(ql.reshape(B, N // DA_BLOCK, DA_BLOCK, DA_H, 2, DA_DK), 1, 0)
    o_lat = lax.map(lambda qb: attend(qb, k_all, v_all), q_blocks)
    o_lat = jnp.moveaxis(o_lat, 0, 1).reshape(B, N, DA_H, 2 * DA_DK)

    def finish(o, gate):
        n = o.shape[1]
        y = rmsnorm(o, subln_w, 1e-5) * (1.0 - lambda_init)
        return (y.reshape(B, n, DA_VW).astype(gate.dtype) * jax.nn.silu(gate)) @ w_out

    out_lat = finish(o_lat, gl)
    out_ctx = finish(attend(qc, kc, vc), gc) if need_ctx else None
    return out_lat, out_ctx


def neighbourhood_mixer(h_lat, h_ctx, w_in, rpb, w_out, need_ctx):
    B, N, _ = h_lat.shape
    rows = N // GRID_W
    wr = min(NA_WR, rows)

    def project(h):
        n = h.shape[1]
        q, k, v, gate = jnp.split(h @ w_in, 4, axis=-1)
        heads = lambda t: t.reshape(B, n, NA_H, NA_DH)
        return heads(q) * (NA_DH ** -0.5), heads(k), heads(v), gate

    ql, kl, vl, gl = project(h_lat)
    qc, kc, vc, gc = project(h_ctx)
    kg = kl.reshape(B, rows, GRID_W, NA_H, NA_DH)
    vg = vl.reshape(B, rows, GRID_W, NA_H, NA_DH)
    qg = jnp.moveaxis(ql.reshape(B, rows, GRID_W, NA_H, NA_DH), 1, 0)
    cols = jnp.arange(GRID_W)
    col_start = jnp.clip(cols - NA_WC // 2, 0, GRID_W - NA_WC)
    col_idx = col_start[:, None] + jnp.arange(NA_WC)
    dc = col_idx - cols[:, None] + (NA_WC - 1)
    row_start = jnp.clip(jnp.arange(rows) - NA_WR // 2, 0, rows - wr)
    n_win = wr * NA_WC

    def row_block(args):
        r, q_r = args
        rs = row_start[r]
        kb = lax.dynamic_slice_in_dim(kg, rs, wr, axis=1)
        vb = lax.dynamic_slice_in_dim(vg, rs, wr, axis=1)
        kn = kb[:, :, col_idx]
        vn = vb[:, :, col_idx]
        s_win = jnp.einsum('bqhd,biqjhd->bhqij', q_r, kn, preferred_element_type=F32)
        dr = rs + jnp.arange(wr) - r + (NA_WR - 1)
        bias = rpb[:, dr[None, :, None], dc[:, None, :]]
        s_win = s_win + bias.astype(F32)[None]
        s_ctx = jnp.einsum('bqhd,bkhd->bhqk', q_r, kc, preferred_element_type=F32)
        s = jnp.concatenate([s_win.reshape(B, NA_H, GRID_W, n_win), s_ctx], axis=-1)
        p = jax.nn.softmax(s, axis=-1)
        p_win = p[..., :n_win].reshape(B, NA_H, GRID_W, wr, NA_WC).astype(vn.dtype)
        o = jnp.einsum('bhqij,biqjhd->bqhd', p_win, vn, preferred_element_type=F32)
        o = o + jnp.einsum('bhqk,bkhd->bqhd', p[..., n_win:].astype(vc.dtype), vc, preferred_element_type=F32)
        return o

    o = lax.map(row_block, (jnp.arange(rows), qg))
    o_lat = jnp.moveaxis(o, 0, 1).reshape(B, N, NA_W)

    def finish(o, gate):
        return (o.astype(gate.dtype) * jax.nn.silu(gate)) @ w_out

    out_lat = finish(o_lat, gl)
    out_ctx = finish(softmax_attend(qc, kc, vc).reshape(B, -1, NA_W), gc) if need_ctx else None
    return out_lat, out_ctx


def setup_inputs(seed: int = 0) -> dict:
    key = jax.random.key(seed)
    ks = iter(jax.random.split(key, 32))
    nrm = lambda shape, scale: jax.random.normal(next(ks), shape, F32) * scale
    n_dn = len(range(0, DEPTH, N_MIXERS))
    n_da = len(range(1, DEPTH, N_MIXERS))
    n_na = len(range(2, DEPTH, N_MIXERS))
    x = nrm((BATCH, SEQ, D_MODEL), 1.0)
    c = nrm((BATCH, D_MODEL), 1.0)
    ctx = nrm((BATCH, CTX_LEN, D_MODEL), 1.0)
    c_ctx = nrm((D_MODEL,), 1.0)
    norm_w = 1.0 + nrm((DEPTH, D_MODEL), 0.02)
    ada_w = nrm((DEPTH, D_MODEL, 3 * D_MODEL), D_MODEL ** -0.5)
    ada_b = nrm((DEPTH, 3 * D_MODEL), 0.02)
    dn_w_in = nrm((n_dn, D_MODEL, DN_PROJ), D_MODEL ** -0.5)
    dn_conv_w = nrm((n_dn, DN_CONV_CH, DN_CONV_K), DN_CONV_K ** -0.5)
    dn_a_log = jnp.log(jax.random.uniform(next(ks), (n_dn, 2, DN_HV), F32, 1.0, 16.0))
    dt = jnp.exp(jax.random.uniform(next(ks), (n_dn, 2, DN_HV), F32, math.log(1e-3), math.log(1e-1)))
    dn_dt_bias = dt + jnp.log(-jnp.expm1(-dt))
    dn_onorm_w = 1.0 + nrm((n_dn, DN_DV), 0.02)
    dn_w_out = nrm((n_dn, DN_VW, D_MODEL), DN_VW ** -0.5)
    da_w_in = nrm((n_da, D_MODEL, DA_PROJ), D_MODEL ** -0.5)
    da_lambda = nrm((n_da, 4, DA_DK), 0.1)
    da_subln_w = 1.0 + nrm((n_da, 2 * DA_DK), 0.02)
    da_w_out = nrm((n_da, DA_VW, D_MODEL), DA_VW ** -0.5)
    na_w_in = nrm((n_na, D_MODEL, NA_PROJ), D_MODEL ** -0.5)
    na_rpb = nrm((n_na, NA_H, 2 * NA_WR - 1, 2 * NA_WC - 1), 0.1)
    na_w_out = nrm((n_na, NA_W, D_MODEL), NA_W ** -0.5)
    final_norm_w = 1.0 + nrm((D_MODEL,), 0.02)
    return {'x': x, 'c': c, 'ctx': ctx, 'c_ctx': c_ctx,
            'norm_w': norm_w, 'ada_w': ada_w, 'ada_b': ada_b,
            'dn_w_in': dn_w_in, 'dn_conv_w': dn_conv_w, 'dn_a_log': dn_a_log, 'dn_dt_bias': dn_dt_bias,
            'dn_onorm_w': dn_onorm_w, 'dn_w_out': dn_w_out,
            'da_w_in': da_w_in, 'da_lambda': da_lambda, 'da_subln_w': da_subln_w, 'da_w_out': da_w_out,
            'na_w_in': na_w_in, 'na_rpb': na_rpb, 'na_w_out': na_w_out,
            'final_norm_w': final_norm_w}


def reference(x, c, ctx, c_ctx, norm_w, ada_w, ada_b,
              dn_w_in, dn_conv_w, dn_a_log, dn_dt_bias, dn_onorm_w, dn_w_out,
              da_w_in, da_lambda, da_subln_w, da_w_out,
              na_w_in, na_rpb, na_w_out, final_norm_w):
    silu_c = jax.nn.silu(c)
    silu_cc = jax.nn.silu(c_ctx)
    for i in range(DEPTH):
        kind, j = i % N_MIXERS, i // N_MIXERS
        need_ctx = i < DEPTH - 1
        mod_l = silu_c @ ada_w[i] + ada_b[i]
        mod_c = silu_cc @ ada_w[i] + ada_b[i]
        sh_l, sc_l, gt_l = jnp.split(mod_l[:, None, :], 3, axis=-1)
        sh_c, sc_c, gt_c = jnp.split(mod_c, 3, axis=-1)
        h_lat = rmsnorm(x, norm_w[i]) * (1.0 + sc_l) + sh_l
        h_ctx = rmsnorm(ctx, norm_w[i]) * (1.0 + sc_c) + sh_c
        if kind == 0:
            out_lat, out_ctx = deltanet_mixer(h_lat, h_ctx, dn_w_in[j], dn_conv_w[j], dn_a_log[j],
                                              dn_dt_bias[j], dn_onorm_w[j], dn_w_out[j], need_ctx)
        elif kind == 1:
            out_lat, out_ctx = diff_attn_mixer(h_lat, h_ctx, da_w_in[j], da_lambda[j], da_subln_w[j],
                                               da_w_out[j], i, need_ctx)
        else:
            out_lat, out_ctx = neighbourhood_mixer(h_lat, h_ctx, na_w_in[j], na_rpb[j], na_w_out[j], need_ctx)
        x = x + gt_l * out_lat.astype(x.dtype)
        if need_ctx:
            ctx = ctx + gt_c * out_ctx.astype(ctx.dtype)
    return rmsnorm(x, final_norm_w)
```

```python
import numpy as np
from contextlib import ExitStack
import concourse.bass as bass
import concourse.mybir as mybir
from concourse.bass_utils import run_bass_kernel_spmd

F32 = mybir.dt.float32
BF16 = mybir.dt.bfloat16
AF = mybir.ActivationFunctionType
ALU = mybir.AluOpType
AX = mybir.AxisListType

D = 2048
KD = 16
ENGS = ["tensor", "vector", "scalar", "gpsimd", "sync"]
GEN = 30000
NDSEM = 24


class Buf:
    __slots__ = ("w", "r")

    def __init__(self):
        self.w = None
        self.r = []


class T:
    def __init__(self, t):
        self.t = t
        self.b = Buf()
        self.subs = {}

    def __getitem__(self, idx):
        return self.t[idx]

    def sub(self, key):
        s = self.subs.get(key)
        if s is None:
            s = self.subs[key] = Buf()
        return s


def _b(x):
    return x.b if isinstance(x, T) else x


class Pool:
    def __init__(self, tiles):
        self.tiles = tiles
        self.i = 0

    def next(self):
        t = self.tiles[self.i]
        self.i = (self.i + 1) % len(self.tiles)
        return t


class Prog:
    def __init__(self, nc, es):
        self.nc = nc
        self.es = es
        self.q = {e: [] for e in ENGS}
        self.cnt = {e: 0 for e in ENGS}
        self.esem = {e: [] for e in ENGS}
        self.known = {e: {} for e in ENGS}
        self.dsem = [es.enter_context(nc.semaphore("dq%d" % i)) for i in range(NDSEM)]
        self.dtot = [0] * NDSEM
        self.dnext = {"sync": 0, "gpsimd": 0, "scalar": 0}
        self.drange = {"sync": (0, NDSEM // 2), "scalar": (0, NDSEM // 2), "gpsimd": (NDSEM // 2, NDSEM)}
        self.nuid = 0

    def uid(self, base):
        self.nuid += 1
        return "%s_%d" % (base, self.nuid)

    def sb(self, scope, name, shape, dt):
        return T(scope.enter_context(self.nc.sbuf_tensor(self.uid(name), list(shape), dt)))

    def ps(self, scope, name, shape, dt):
        return T(scope.enter_context(self.nc.psum_tensor(self.uid(name), list(shape), dt)))

    def pool(self, scope, name, shape, dt, n, psum=False):
        f = self.ps if psum else self.sb
        return Pool([f(scope, name, shape, dt) for _ in range(n)])

    def _sem_for(self, eng, seq):
        g = (seq - 1) // GEN
        while len(self.esem[eng]) <= g:
            self.esem[eng].append(
                self.es.enter_context(self.nc.semaphore("e_%s_%d" % (eng, len(self.esem[eng])))))
        return self.esem[eng][g], seq - g * GEN

    def _waits(self, eng, deps):
        out = []
        kn = self.known[eng]
        best = {}
        for tok in deps:
            key = tok[:2]
            if tok[2] > best.get(key, 0):
                best[key] = tok[2]
        for key, val in best.items():
            if kn.get(key, 0) >= val:
                continue
            if key[0] == "e":
                if key[1] == eng and eng == "tensor":
                    continue
                sem, v = self._sem_for(key[1], val)
                out.append((sem, v))
            else:
                out.append((self.dsem[key[1]], val))
            kn[key] = val
        return out

    def _collect(self, reads, writes, same_eng=None):
        deps = set()
        for b in reads:
            b = _b(b)
            if b.w is not None:
                deps.add(b.w)
        for b in writes:
            b = _b(b)
            if b.w is not None:
                deps.add(b.w)
            for r in b.r:
                deps.add(r)
        return deps

    def _record(self, tok, reads, writes):
        for b in reads:
            b = _b(b)
            b.r = [r for r in b.r if r[:2] != tok[:2]] + [tok]
        for b in writes:
            b = _b(b)
            b.w = tok
            b.r = []

    def op(self, eng, fn, reads=(), writes=()):
        deps = self._collect(reads, writes, same_eng=eng)
        waits = self._waits(eng, deps)
        seq = self.cnt[eng] + 1
        self.cnt[eng] = seq
        sem, _ = self._sem_for(eng, seq)
        self.q[eng].append((waits, fn, sem, 1))
        self._record(("e", eng, seq), reads, writes)

    def dma(self, eng, out, in_, reads=(), writes=(), **kw):
        lo, hi = self.drange[eng]
        qk = "gpsimd" if eng == "gpsimd" else "sync"
        idx = lo + self.dnext[qk]
        self.dnext[qk] = (self.dnext[qk] + 1) % (hi - lo)
        deps = self._collect(reads, writes)
        if self.dtot[idx] > 0:
            deps.add(("d", idx, self.dtot[idx]))
        waits = self._waits(eng, deps)
        self.dtot[idx] += 16
        tok = ("d", idx, self.dtot[idx])
        self.q[eng].append((waits, (lambda e: e.dma_start(out=out, in_=in_, **kw)), self.dsem[idx], 16))
        self._record(tok, reads, writes)

    def barrier(self):
        deps = set()
        for e in ENGS:
            if self.cnt[e] > 0:
                deps.add(("e", e, self.cnt[e]))
        for i in range(NDSEM):
            if self.dtot[i] > 0:
                deps.add(("d", i, self.dtot[i]))
        for e in ENGS:
            d2 = set(t for t in deps if not (t[0] == "e" and t[1] == e))
            waits = self._waits(e, d2)
            if waits:
                self.q[e].append((waits, None, None, 0))

    def emit(self, blk):
        for eng in ENGS:
            items = self.q[eng]
            if not items:
                continue

            def body(e, items=items):
                for waits, fn, sem, inc in items:
                    for s, v in waits:
                        e.wait_ge(s, v)
                    if fn is not None:
                        fn(e).then_inc(sem, inc)

            getattr(blk, eng)(body)

    def mm(self, out, lhsT, rhs, start, stop, reads, writes):
        self.op("tensor", lambda e: e.matmul(out, lhsT=lhsT, rhs=rhs, start=start, stop=stop),
                reads, writes)

    def tr(self, out, in_, ident, reads, writes):
        self.op("tensor", lambda e: e.transpose(out=out, in_=in_, identity=ident), reads, writes)

    def act(self, out, in_, func, reads, writes, eng="scalar", **kw):
        self.op(eng, lambda e: e.activation(out=out, in_=in_, func=func, **kw), reads, writes)

    def ts(self, eng, out, in0, s1, s2, op0, op1, reads, writes):
        if op1 is None:
            self.op(eng, lambda e: e.tensor_scalar(out=out, in0=in0, scalar1=s1, scalar2=None, op0=op0),
                    reads, writes)
        else:
            self.op(eng, lambda e: e.tensor_scalar(out=out, in0=in0, scalar1=s1, scalar2=s2, op0=op0, op1=op1),
                    reads, writes)

    def tt(self, eng, out, in0, in1, op, reads, writes):
        self.op(eng, lambda e: e.tensor_tensor(out=out, in0=in0, in1=in1, op=op), reads, writes)

    def stt(self, out, in0, scalar, in1, op0, op1, reads, writes):
        self.op("vector", lambda e: e.scalar_tensor_tensor(out=out, in0=in0, scalar=scalar, in1=in1,
                                                           op0=op0, op1=op1), reads, writes)

    def cp(self, eng, out, in_, reads, writes):
        if eng == "scalar":
            self.op(eng, lambda e: e.activation(out=out, in_=in_, func=AF.Copy), reads, writes)
        else:
            self.op(eng, lambda e: e.tensor_copy(out=out, in_=in_), reads, writes)


CONST_NAMES = ["ident", "ones", "triF", "triB", "ntriF", "ntriB", "strF", "strB",
               "mtsF", "mtsB", "mtiF", "mtiB", "m0", "m8", "m16", "m32", "m64", "rotm"]


def make_consts():
    c = np.arange(128)[:, None]
    i = np.arange(128)[None, :]
    m = {}
    m["ident"] = (c == i)
    m["ones"] = np.ones((128, 128), bool)
    m["triF"] = (c <= i)
    m["triB"] = (c >= i)
    m["ntriF"] = ~(c <= i)
    m["ntriB"] = ~(c >= i)
    m["strF"] = (c > i)
    m["strB"] = (c < i)
    m["mtsF"] = (i > c)
    m["mtsB"] = (i < c)
    m["mtiF"] = (i >= c)
    m["mtiB"] = (i <= c)
    m["m0"] = (c // 8 == i // 8)
    for sz in (8, 16, 32, 64):
        m["m%d" % sz] = (c // (2 * sz) == i // (2 * sz)) & (c // sz != i // sz)
    rot = np.zeros((128, 128), np.float32)
    for d in range(128):
        if (d // 32) % 2 == 0:
            rot[d + 32, d] = -1.0
        else:
            rot[d - 32, d] = 1.0
    m["rotm"] = rot
    return np.concatenate([m[n].astype(np.float32) for n in CONST_NAMES], axis=1)


class Ctx:
    pass


def build(cfg):
    NCTX, NLAT = cfg["NCTX"], cfg["NLAT"]
    NTOK = NCTX + NLAT
    NT = NTOK // 128
    NTC = NCTX // 128
    layers = cfg["layers"]
    NL = max(len(layers), 1)
    dbg = cfg.get("debug", [])
    nc = bass.Bass("TRN2", target_bir_lowering=False)
    C = Ctx()
    C.cfg, C.NCTX, C.NLAT, C.NTOK, C.NT, C.NTC = cfg, NCTX, NLAT, NTOK, NT, NTC
    C.nc = nc

    def din(name, shape, dt=F32):
        return nc.dram_tensor(name, list(shape), dt, kind="ExternalInput").ap()

    def dint(name, shape, dt=F32):
        kind = "ExternalOutput" if name in cfg.get("debug_out", []) else "Internal"
        return nc.dram_tensor(name, list(shape), dt, kind=kind).ap()

    def dout(name, shape, dt=F32):
        return nc.dram_tensor(name, list(shape), dt, kind="ExternalOutput").ap()

    C.dint, C.dout = dint, dout
    I = C.I = {}
    I["xs"] = din("xs", [NTOK, D])
    I["cvec"] = din("cvec", [2, D])
    I["consts"] = din("consts", [128, 128 * len(CONST_NAMES)])
    I["norm_w"] = din("norm_w", [NL, D])
    I["ada_w"] = din("ada_w", [NL, D, 3 * D])
    I["ada_b"] = din("ada_b", [NL, 3 * D])
    I["final_norm_w"] = din("final_norm_w", [D])
    for li, l in enumerate(layers):
        kind = l % 3
        if kind == 0:
            I["w_in%d" % li] = din("w_in%d" % li, [D, 12416])
            I["conv_w%d" % li] = din("conv_w%d" % li, [8192, 5])
            I["a_log%d" % li] = din("a_log%d" % li, [64])
            I["dt_bias%d" % li] = din("dt_bias%d" % li, [64])
            I["onorm_w%d" % li] = din("onorm_w%d" % li, [128])
            I["w_out%d" % li] = din("w_out%d" % li, [4096, D])
        elif kind == 1:
            I["w_in%d" % li] = din("w_in%d" % li, [D, 8192])
            I["lam%d" % li] = din("lam%d" % li, [512])
            I["subln%d" % li] = din("subln%d" % li, [256])
            I["w_out%d" % li] = din("w_out%d" % li, [2048, D])
            I["rope%d" % li] = din("rope%d" % li, [128, 2, NLAT])
        else:
            I["w_in%d" % li] = din("w_in%d" % li, [D, 8192])
            I["nab%d" % li] = din("nab%d" % li, [16, 128, 19, 64])
            I["w_out%d" % li] = din("w_out%d" % li, [2048, D])
    C.out = dout("out", [NLAT, D])
    C.xa = dint("xa", [NTOK, D])
    C.xb = dint("xb", [NTOK, D])
    C.gate_row = dint("gate_row", [NL, 2, D])
    C.dbg = {}

    with ExitStack() as es:
        P = Prog(nc, es)
        C.P = P
        C.consts = P.sb(es, "consts", [128, 128 * len(CONST_NAMES)], F32)
        C.cbf = P.sb(es, "cbf", [128, 128 * len(CONST_NAMES)], BF16)
        C.gainT = P.sb(es, "gainT", [128, NL, KD, 2], F32)
        C.shiftT = P.sb(es, "shiftT", [128, NL, KD, 2], F32)
        P.dma("sync", C.consts[:], I["consts"][:, :], writes=[C.consts])
        P.cp("vector", C.cbf[:], C.consts[:], [C.consts], [C.cbf])

        phase_mod(C)
        P.barrier()
        xin = I["xs"]
        xouts = [C.xa, C.xb]
        for li, l in enumerate(layers):
            xout = xouts[li % 2]
            kind = l % 3
            if kind == 0:
                dn_layer(C, li, l, xin, xout)
            elif kind == 1:
                da_layer(C, li, l, xin, xout)
            else:
                na_layer(C, li, l, xin, xout)
            P.barrier()
            xin = xout
        if cfg.get("final", True):
            phase_final(C, xin)
        P.barrier()
        blk = es.enter_context(nc.Block())
        P.emit(blk)
    return nc


def cmat(C, name, bf=False):
    i = CONST_NAMES.index(name)
    t = C.cbf if bf else C.consts
    return t[:, i * 128:(i + 1) * 128]


def phase_mod(C):
    P, I = C.P, C.I
    NL = len(C.cfg["layers"])
    if NL == 0:
        return
    with ExitStack() as s:
        cT = P.sb(s, "cT", [128, KD, 2], F32)
        scT = P.sb(s, "scT", [128, KD, 2], F32)
        for w in range(2):
            src = I["cvec"][w, :].rearrange("(k p o) -> p k o", p=128, o=1)
            P.dma("sync", cT[:, :, w:w + 1], src, writes=[cT], allow_slow_non_contiguous=True)
        P.act(scT[:], cT[:], AF.Silu, [cT], [scT])
        wpool = P.pool(s, "adaw", [128, KD, 512], F32, 2)
        pmod = P.ps(s, "pmod", [128, 96], F32)
        modsb = P.sb(s, "modsb", [128, 48, 2], F32)
        abT = P.sb(s, "abT", [128, 48], F32)
        nwT = P.sb(s, "nwT", [128, KD], F32)
        for li in range(NL):
            P.dma("sync", abT[:].unsqueeze(2) if False else abT[:, :],
                  I["ada_b"][li, :].rearrange("(m p) -> p m", p=128), writes=[abT],
                  allow_slow_non_contiguous=True)
            P.dma("sync", nwT[:, :], I["norm_w"][li, :].rearrange("(k p) -> p k", p=128), writes=[nwT],
                  allow_slow_non_contiguous=True)
            for cg in range(12):
                wg = wpool.next()
                P.dma("sync", wg[:], I["ada_w"][li, :, cg * 512:(cg + 1) * 512].rearrange(
                    "(k p) c -> p k c", p=128), writes=[wg])
                for m in range(4):
                    r0 = (cg * 4 + m) * 2
                    for k in range(KD):
                        P.mm(pmod[:, r0:r0 + 2], wg[:, k, m * 128:(m + 1) * 128], scT[:, k, :],
                             k == 0, k == KD - 1, [wg, scT], [pmod])
            pm = pmod[:, :].rearrange("p (m w) -> p m w", w=2)
            for w in range(2):
                P.tt("vector", modsb[:, :, w], pm[:, :, w], abT[:, :], ALU.add, [pmod, abT], [modsb])
            for w in range(2):
                P.stt(C.gainT[:, li, :, w], modsb[:, 16:32, w], 1.0, nwT[:, :], ALU.add, ALU.mult,
                      [modsb, nwT], [C.gainT])
                P.cp("vector", C.shiftT[:, li, :, w], modsb[:, 0:16, w], [modsb], [C.shiftT])
                P.dma("sync", C.gate_row[li, w, :].rearrange("(k p o) -> p k o", p=128, o=1),
                      modsb[:, 32:48, w:w + 1], reads=[modsb], allow_slow_non_contiguous=True)


def phase_norm(C, li, xin):
    P = C.P
    with ExitStack() as s:
        xpool = P.pool(s, "xt", [128, D], F32, 2)
        xnpool = P.pool(s, "xn", [128, D], F32, 2)
        sqj = P.sb(s, "sqj", [128, D], BF16)
        stat = P.pool(s, "stat", [128, 4], F32, 2)
        pspool = P.pool(s, "pst", [128, 512], F32, 4, psum=True)
        ident = cmat(C, "ident")
        for tt in range(C.NT):
            w = 1 if tt < C.NTC else 0
            xt = xpool.next()
            xn = xnpool.next()
            st = stat.next()
            P.dma("sync", xt[:], xin[tt * 128:(tt + 1) * 128, :], writes=[xt])
            P.act(sqj[:], xt[:], AF.Square, [xt], [sqj, st], accum_out=st[:, 0:1])
            P.ts("vector", st[:, 1:2], st[:, 0:1], 1.0 / D, 1e-6, ALU.mult, ALU.add, [st], [st])
            P.act(st[:, 2:3], st[:, 1:2], AF.Sqrt, [st], [st])
            P.op("vector", lambda e, st=st: e.reciprocal(out=st[:, 3:4], in_=st[:, 2:3]), [st], [st])
            P.act(xn[:], xt[:], AF.Identity, [xt, st], [xn], scale=st[:, 3:4])
            stop = C.cfg.get("norm_stop", 9)
            if stop <= 1:
                continue
            for k4 in range(4):
                ps = pspool.next()
                for j in range(4):
                    k = k4 * 4 + j
                    P.tr(ps[:, j * 128:(j + 1) * 128], xn[:, k * 128:(k + 1) * 128], ident,
                         [xn, C.consts], [ps])
                for j in range(4):
                    if stop <= 2:
                        continue
                    if stop <= 3 and j % 2 == 1:
                        continue
                    k = k4 * 4 + j
                    dst = C.hT[:, k, tt * 128:(tt + 1) * 128]
                    hb = C.hT.sub((tt, k))
                    g = C.gainT[:, li, k, w:w + 1]
                    sh = C.shiftT[:, li, k, w:w + 1]
                    P.ts("vector", dst, ps[:, j * 128:(j + 1) * 128], g, sh, ALU.mult, ALU.add,
                         [ps, C.gainT, C.shiftT], [hb])


def hT_bufs(C, t0, t1):
    return [C.hT.sub((tt, k)) for tt in range(t0 // 128, (t1 + 127) // 128) for k in range(KD)]


def phase_final(C, xin):
    P, I = C.P, C.I
    with ExitStack() as s:
        xpool = P.pool(s, "fx", [128, D], F32, 2)
        opool = P.pool(s, "fo", [128, D], F32, 2)
        sqj = P.sb(s, "fsq", [128, D], BF16)
        stat = P.pool(s, "fstat", [128, 4], F32, 2)
        fw = P.sb(s, "fw", [128, D], F32)
        P.dma("sync", fw[:], I["final_norm_w"].partition_broadcast(128), writes=[fw])
        for tt in range(C.NTC, C.NT):
            xt = xpool.next()
            ot = opool.next()
            st = stat.next()
            P.dma("sync", xt[:], xin[tt * 128:(tt + 1) * 128, :], writes=[xt])
            P.act(sqj[:], xt[:], AF.Square, [xt], [sqj, st], accum_out=st[:, 0:1])
            P.ts("vector", st[:, 1:2], st[:, 0:1], 1.0 / D, 1e-6, ALU.mult, ALU.add, [st], [st])
            P.act(st[:, 2:3], st[:, 1:2], AF.Sqrt, [st], [st])
            P.op("vector", lambda e, st=st: e.reciprocal(out=st[:, 3:4], in_=st[:, 2:3]), [st], [st])
            P.stt(ot[:], xt[:], st[:, 3:4], fw[:], ALU.mult, ALU.mult, [xt, st, fw], [ot])
            r0 = (tt - C.NTC) * 128
            P.dma("sync", C.out[r0:r0 + 128, :], ot[:], reads=[ot])


def dump(C, name, tile, shape, dt):
    ap = C.dout("dbg_" + name, shape, dt)
    C.P.dma("sync", ap, tile[:], reads=[tile])


def make_in_map(inp, b, cfg):
    layers = cfg["layers"]
    m = {}
    m["xs"] = np.ascontiguousarray(np.concatenate([inp["ctx"][b], inp["x"][b]], axis=0), dtype=np.float32)
    m["cvec"] = np.ascontiguousarray(np.stack([inp["c"][b], inp["c_ctx"]], axis=0), dtype=np.float32)
    m["consts"] = make_consts()
    m["norm_w"] = np.ascontiguousarray(np.stack([inp["norm_w"][l] for l in layers]) if layers else np.zeros((1, D), np.float32))
    m["ada_w"] = np.ascontiguousarray(np.stack([inp["ada_w"][l] for l in layers]) if layers else np.zeros((1, D, 3 * D), np.float32))
    m["ada_b"] = np.ascontiguousarray(np.stack([inp["ada_b"][l] for l in layers]) if layers else np.zeros((1, 3 * D), np.float32))
    m["final_norm_w"] = np.ascontiguousarray(inp["final_norm_w"], dtype=np.float32)
    for li, l in enumerate(layers):
        kind, j = l % 3, l // 3
        if kind == 0:
            m["w_in%d" % li] = np.ascontiguousarray(inp["dn_w_in"][j])
            m["conv_w%d" % li] = np.ascontiguousarray(inp["dn_conv_w"][j])
            m["a_log%d" % li] = np.ascontiguousarray(inp["dn_a_log"][j].reshape(64))
            m["dt_bias%d" % li] = np.ascontiguousarray(inp["dn_dt_bias"][j].reshape(64))
            m["onorm_w%d" % li] = np.ascontiguousarray(inp["dn_onorm_w"][j])
            m["w_out%d" % li] = np.ascontiguousarray(inp["dn_w_out"][j])
        elif kind == 1:
            m["w_in%d" % li] = np.ascontiguousarray(inp["da_w_in"][j])
            m["lam%d" % li] = np.ascontiguousarray(inp["da_lambda"][j].reshape(512))
            m["subln%d" % li] = np.ascontiguousarray(inp["da_subln_w"][j])
            m["w_out%d" % li] = np.ascontiguousarray(inp["da_w_out"][j])
            m["rope%d" % li] = rope_table(cfg["NLAT"])
        else:
            m["w_in%d" % li] = np.ascontiguousarray(inp["na_w_in"][j])
            m["nab%d" % li] = na_bias_table(inp["na_rpb"][j])
            m["w_out%d" % li] = np.ascontiguousarray(inp["na_w_out"][j])
    return m


NEG = -30000.0


def rope_table(nlat):
    t = np.arange(nlat)
    row = (t // 64).astype(np.float32)
    col = (t % 64).astype(np.float32)
    inv = (np.float32(10000.0) ** (-np.arange(0, 64, 2, dtype=np.float32) / np.float32(64))).astype(np.float32)
    ang_r = row[:, None] * inv
    ang_c = col[:, None] * inv
    ang = np.concatenate([ang_r, ang_r, ang_c, ang_c], axis=-1)
    return np.ascontiguousarray(np.stack([np.cos(ang).T, np.sin(ang).T], axis=1).astype(np.float32))


def na_bias_table(rpb):
    H = rpb.shape[0]
    kc = np.arange(64)[:, None]
    qc = np.arange(64)[None, :]
    cs = np.clip(qc - 8, 0, 48)
    valid = (kc >= cs) & (kc < cs + 16)
    dc = np.clip(kc - qc + 15, 0, 30)
    B1 = np.where(valid[None, None], rpb[:, :, dc], np.float32(NEG)).astype(np.float32)
    negt = np.full((H, 64, 64), NEG, np.float32)
    tiles = []
    for dr in range(14):
        tiles.append(np.concatenate([B1[:, dr], B1[:, dr + 1]], axis=1))
    tiles.append(np.concatenate([negt, B1[:, 3]], axis=1))
    for dr in (4, 6, 8):
        tiles.append(np.concatenate([B1[:, dr], B1[:, dr + 1]], axis=1))
    tiles.append(np.concatenate([B1[:, 10], negt], axis=1))
    return np.ascontiguousarray(np.stack(tiles, axis=2).astype(np.float32))


def kernel(**inp):
    inp = {k: np.asarray(v) for k, v in inp.items()}
    B, NLAT, _ = inp["x"].shape
    NCTX = inp["ctx"].shape[1]
    cfg = dict(NCTX=NCTX, NLAT=NLAT, layers=[0, 1, 2, 3], final=True)
    nc = build(cfg)
    in_maps = [make_in_map(inp, b, cfg) for b in range(B)]
    res = run_bass_kernel_spmd(nc, in_maps, core_ids=list(range(B)))
    return np.stack([np.asarray(res.results[b]["out"]) for b in range(B)], axis=0).astype(np.float32)


class QPool:
    def __init__(self, P, scope, name, nbanks, dt=F32):
        self.items = []
        nq = 4 if dt == F32 else 8
        banks = [P.ps(scope, name, [128, 128 * nq], dt) for b in range(nbanks)]
        for q in range(nq):
            for t in banks:
                self.items.append((t[:, q * 128:(q + 1) * 128], t.b))
        self.i = 0

    def next(self):
        it = self.items[self.i]
        self.i = (self.i + 1) % len(self.items)
        return it


def phase_outproj(C, li, w_out, KB, yT, xin, xout):
    P = C.P
    NTOK, NT = C.NTOK, C.NT
    with ExitStack() as s:
        wsb = P.sb(s, "wo", [128, KB, 1024], BF16)
        gbc = [P.sb(s, "gbc", [128, 1024], F32) for _ in range(2)]
        ypool = P.pool(s, "yt", [128, KB, 512], BF16, 2)
        xpool = P.pool(s, "xo", [128, 1024], F32, 2)
        opool = P.pool(s, "oo", [128, 1024], F32, 2)
        tpool = P.pool(s, "ot", [128, 512], F32, 2)
        psp = P.pool(s, "pso", [128, 512], F32, 4, psum=True)
        for ch in range(2):
            c0 = ch * 1024
            for kb in range(0, KB, 4):
                P.dma("gpsimd", wsb[:, kb:kb + 4, :],
                      w_out[kb * 128:(kb + 4) * 128, c0:c0 + 1024].rearrange("(k p) c -> p k c", p=128),
                      writes=[wsb])
            for w in range(2):
                P.dma("sync", gbc[w][:], C.gate_row[li, w, c0:c0 + 1024].partition_broadcast(128),
                      writes=[gbc[w]])
            for t0 in range(0, NTOK, 512):
                tw = min(512, NTOK - t0)
                yt = ypool.next()
                P.dma("sync", yt[:, :, :tw], yT[:, t0:t0 + tw].rearrange("(k p) t -> p k t", p=128),
                      writes=[yt])
                for ts_ in range(0, tw, 128):
                    tok0 = t0 + ts_
                    w = 1 if tok0 < C.NCTX else 0
                    xt = xpool.next()
                    ot = opool.next()
                    P.dma("sync", xt[:], xin[tok0:tok0 + 128, c0:c0 + 1024], writes=[xt])
                    for cg in range(2):
                        ps = psp.next()
                        for k in range(KB):
                            P.mm(ps[:], yt[:, k, ts_:ts_ + 128], wsb[:, k, cg * 512:(cg + 1) * 512],
                                 k == 0, k == KB - 1, [yt, wsb], [ps])
                        tmp = tpool.next()
                        P.tt("vector", tmp[:], ps[:], gbc[w][:, cg * 512:(cg + 1) * 512], ALU.mult,
                             [ps, gbc[w]], [tmp])
                        P.tt("gpsimd", ot[:, cg * 512:(cg + 1) * 512], tmp[:], xt[:, cg * 512:(cg + 1) * 512],
                             ALU.add, [tmp, xt], [ot])
                    P.dma("sync", xout[tok0:tok0 + 128, c0:c0 + 1024], ot[:], reads=[ot])


def dn_layer(C, li, l, xin, xout):
    P, I, nc = C.P, C.I, C.nc
    NTOK, NT, NTC, NCTX, NLAT = C.NTOK, C.NT, C.NTC, C.NCTX, C.NLAT
    dbg = C.cfg.get("debug", [])
    w_in = I["w_in%d" % li]
    pre = C.dint("pre%d" % li, [8192, NTOK])
    zs = C.dint("zs%d" % li, [NTOK, 4096])
    ba = C.dint("ba%d" % li, [NTOK, 128])
    qT = C.dint("qT%d" % li, [16, 128, NTOK], BF16)
    kT = C.dint("kT%d" % li, [16, 128, NTOK], BF16)
    ktok = C.dint("ktok%d" % li, [16, NTOK, 128], BF16)
    vtok = C.dint("vtok%d" % li, [32, NTOK, 128], BF16)
    od = C.dint("od%d" % li, [2, 32, NTOK, 128])
    yT = C.dint("yT%d" % li, [4096, NTOK], BF16)
    stop = C.cfg.get("dn_stop", 99)

    with ExitStack() as s:
        C.hT = P.sb(s, "hT", [128, KD, NTOK], BF16)
        phase_norm(C, li, xin)
        P.barrier()
        if "hT" in dbg and li == 0:
            dump(C, "hT", C.hT, [128, KD, NTOK], BF16)
        with ExitStack() as s2:
            wfm = P.pool(s2, "wfm", [128, KD, 128], BF16, 3)
            stage = P.pool(s2, "stage", [128, NTOK], F32, 2)
            psp = P.pool(s2, "pp", [128, 512], F32, 4, psum=True)
            ev = 0
            for m in range(64):
                wt = wfm.next()
                P.dma("gpsimd", wt[:], w_in[:, m * 128:(m + 1) * 128].rearrange("(k p) c -> p k c", p=128),
                      writes=[wt])
                st = stage.next()
                for t0 in range(0, NTOK, 512):
                    tw = min(512, NTOK - t0)
                    ps = psp.next()
                    for k in range(KD):
                        P.mm(ps[:, :tw], wt[:, k, :], C.hT[:, k, t0:t0 + tw], k == 0, k == KD - 1,
                             [wt] + [C.hT.sub((tt, k)) for tt in range(t0 // 128, (t0 + tw) // 128)], [ps])
                    P.cp("scalar" if ev % 2 == 0 else "vector", st[:, t0:t0 + tw], ps[:, :tw], [ps], [st])
                    ev += 1
                P.dma("sync", pre[m * 128:(m + 1) * 128, :], st[:], reads=[st])
        P.barrier()
        with ExitStack() as s2:
            wtm = P.pool(s2, "wtm", [128, KD, 512], BF16, 2)
            zst = P.pool(s2, "zst", [128, 512], F32, 3)
            psp = P.pool(s2, "pp2", [128, 512], F32, 4, psum=True)
            for g in range(9):
                cw = 512 if g < 8 else 128
                c0 = 8192 + g * 512
                wt = wtm.next()
                P.dma("gpsimd", wt[:, :, :cw], w_in[:, c0:c0 + cw].rearrange("(k p) c -> p k c", p=128),
                      writes=[wt])
                for tt in range(NT):
                    ps = psp.next()
                    for k in range(KD):
                        P.mm(ps[:, :cw], C.hT[:, k, tt * 128:(tt + 1) * 128], wt[:, k, :cw], k == 0, k == KD - 1,
                             [wt, C.hT.sub((tt, k))], [ps])
                    zt = zst.next()
                    if g < 8:
                        P.act(zt[:, :cw], ps[:, :cw], AF.Silu, [ps], [zt])
                        P.dma("sync", zs[tt * 128:(tt + 1) * 128, g * 512:(g + 1) * 512], zt[:, :cw], reads=[zt])
                    else:
                        P.cp("vector", zt[:, :cw], ps[:, :cw], [ps], [zt])
                        P.dma("sync", ba[tt * 128:(tt + 1) * 128, :], zt[:, :cw], reads=[zt])
    P.barrier()
    if stop <= 1:
        return

    L = NTOK + 2
    with ExitStack() as s:
        cw_sb = P.sb(s, "cw", [128, 64, 5], F32)
        P.dma("sync", cw_sb[:], I["conv_w%d" % li].rearrange("(c p) j -> p c j", p=128), writes=[cw_sb])
        ppool = P.pool(s, "cpad", [128, NTOK + 6], F32, 2)
        for t in ppool.tiles:
            P.op("gpsimd", lambda e, t=t: e.memset(t[:], 0.0), [], [t])
        acc_p = P.pool(s, "cacc", [128, L], F32, 2)
        post_p = P.pool(s, "cpost", [128, NTOK], F32, 2)
        sq_p = P.pool(s, "csq", [128, NTOK], F32, 1)
        rn_p = P.pool(s, "crn", [128, 512], F32, 2)
        fbf_p = P.pool(s, "cfbf", [128, NTOK], BF16, 2)
        tk_p = P.pool(s, "ctk", [128, NT, 128], BF16, 2)
        ps_n = P.pool(s, "psn", [128, 512], F32, 2, psum=True)
        ps_t = P.pool(s, "pst", [128, 1024], BF16, 2, psum=True)
        ones = cmat(C, "ones")
        identb = cmat(C, "ident", bf=True)
        for cc in range(64):
            pb = ppool.next()
            P.dma("sync", pb[:, 2:2 + NCTX], pre[cc * 128:(cc + 1) * 128, 0:NCTX], writes=[pb])
            P.dma("sync", pb[:, NCTX + 4:NCTX + 4 + NLAT], pre[cc * 128:(cc + 1) * 128, NCTX:NTOK], writes=[pb])
            acc = acc_p.next()
            P.ts("vector", acc[:], pb[:, 0:L], cw_sb[:, cc, 0:1], None, ALU.mult, None, [pb, cw_sb], [acc])
            for j in range(1, 5):
                P.stt(acc[:], pb[:, j:j + L], cw_sb[:, cc, j:j + 1], acc[:], ALU.mult, ALU.add,
                      [pb, cw_sb, acc], [acc])
            fbf = fbf_p.next()
            if cc < 32:
                post = post_p.next()
                P.act(post[:, 0:NCTX], acc[:, 0:NCTX], AF.Silu, [acc], [post])
                P.act(post[:, NCTX:NTOK], acc[:, NCTX + 2:NCTX + 2 + NLAT], AF.Silu, [acc], [post])
                sq = sq_p.next()
                P.tt("gpsimd", sq[:], post[:], post[:], ALU.mult, [post], [sq])
                qscale = (128.0 ** -0.5) if cc < 16 else 1.0
                for t0 in range(0, NTOK, 512):
                    tw = min(512, NTOK - t0)
                    ps = ps_n.next()
                    P.mm(ps[:, :tw], ones, sq[:, t0:t0 + tw], True, True, [C.consts, sq], [ps])
                    rn = rn_p.next()
                    P.ts("vector", rn[:, :tw], ps[:, :tw], 1e-6, None, ALU.add, None, [ps], [rn])
                    P.act(rn[:, :tw], rn[:, :tw], AF.Sqrt, [rn], [rn])
                    P.op("vector", lambda e, rn=rn, tw=tw: e.reciprocal(out=rn[:, :tw], in_=rn[:, :tw]), [rn], [rn])
                    P.stt(fbf[:, t0:t0 + tw], post[:, t0:t0 + tw], qscale, rn[:, :tw], ALU.mult, ALU.mult,
                          [post, rn], [fbf])
                h = cc % 16
                P.dma("sync", (qT if cc < 16 else kT)[h, :, :], fbf[:], reads=[fbf])
            else:
                P.act(fbf[:, 0:NCTX], acc[:, 0:NCTX], AF.Silu, [acc], [fbf])
                P.act(fbf[:, NCTX:NTOK], acc[:, NCTX + 2:NCTX + 2 + NLAT], AF.Silu, [acc], [fbf])
            if cc >= 16:
                tk = tk_p.next()
                for t8 in range(0, NT, 8):
                    n8 = min(8, NT - t8)
                    ps = ps_t.next()
                    for j in range(n8):
                        tt = t8 + j
                        P.tr(ps[:, j * 128:(j + 1) * 128], fbf[:, tt * 128:(tt + 1) * 128], identb,
                             [fbf, C.cbf], [ps])
                    P.cp("scalar" if (t8 // 8) % 2 == 0 else "vector",
                         tk[:, t8:t8 + n8, :], ps[:, :n8 * 128].rearrange("p (n e) -> p n e", e=128), [ps], [tk])
                dst = ktok[cc - 16] if cc < 32 else vtok[cc - 32]
                P.dma("sync", dst.rearrange("(n p) e -> p n e", p=128), tk[:], reads=[tk])
    P.barrier()
    if stop <= 2:
        return
    dn_scan(C, li, ba, qT, kT, ktok, vtok, od)
    P.barrier()
    if stop <= 3:
        return
    dn_epilogue(C, li, od, zs, yT)
    P.barrier()
    if stop <= 4:
        return
    phase_outproj(C, li, I["w_out%d" % li], 32, yT, xin, xout)


def dn_unit(C, W, G, H, qa, qd, qb, hq, hv_i, d, n, S, Sbf, Gm, ATm, od):
    P = C.P
    sfx = "F" if d == 0 else "B"
    tri = cmat(C, "tri" + sfx)
    strc = cmat(C, "str" + sfx)
    ident = cmat(C, "ident")
    h = 2 * hq + hv_i
    col = d * 32 + h
    gcol = G.g_all[:, n, col:col + 1]
    beta = G.beta[:, n, col:col + 1]
    nbeta = G.nbeta[:, n, col:col + 1]
    egc = G.eg[:, n, d, h:h + 1]
    egr = G.eg[:, n, d, 32 + h:33 + h]
    egt = G.eg[:, n, d, 64 + h:65 + h]
    nsl = slice(n * 128, (n + 1) * 128)
    gS = W["gS"].next()
    P.act(gS[:], strc, AF.Identity, [C.consts, G.g_all], [gS], scale=gcol)
    yield
    pD, bD = qa.next()
    P.mm(pD, gS[:], tri, True, True, [gS, C.consts], [bD])
    dec = W["dec"].next()
    P.act(dec[:], pD, AF.Exp, [bD], [dec])
    yield
    X = W["Xb"].next()
    P.stt(X[:], Gm[:], beta, dec[:], ALU.mult, ALU.mult, [Gm, G.beta, dec], [X])
    AI = W["AI"].next()
    P.tt("gpsimd", AI[:], ATm[:], dec[:], ALU.mult, [ATm, dec], [AI])
    yield
    identb = cmat(C, "ident", bf=True)
    pL, bL = qb.next()
    P.tr(pL, X[:], identb, [X, C.cbf], [bL])
    Lf = W["Lfb"].next()
    P.cp("scalar", Lf[:], pL, [bL], [Lf])
    Xd = W["Xb"].next()
    P.tt("gpsimd", Xd[:], X[:], cmat(C, "m0"), ALU.mult, [X, C.consts], [Xd])
    yield
    Ld = W["Lb"].next()
    P.tt("gpsimd", Ld[:], Lf[:], cmat(C, "m0"), ALU.mult, [Lf, C.consts], [Ld])
    R = W["Rb"].next()
    P.tt("gpsimd", R[:], ident, Xd[:], ALU.subtract, [C.consts, Xd], [R])
    yield
    pa, ba_ = qa.next()
    P.mm(pa, Ld[:], Xd[:], True, True, [Ld, Xd], [ba_])
    Xd2 = W["Xb"].next()
    P.cp("scalar", Xd2[:], pa, [ba_], [Xd2])
    pb, bb = qa.next()
    P.mm(pb, Xd[:], Ld[:], True, True, [Xd, Ld], [bb])
    Ld2 = W["Lb"].next()
    P.cp("scalar", Ld2[:], pb, [bb], [Ld2])
    yield
    pc, bc = qd.next()
    P.mm(pc, Ld2[:], R[:], True, True, [Ld2, R], [bc])
    R1 = W["Rb"].next()
    P.tt("vector", R1[:], pc, R[:], ALU.add, [bc, R], [R1])
    pd, bd = qa.next()
    P.mm(pd, Xd2[:], Ld2[:], True, True, [Xd2, Ld2], [bd])
    Ld4 = W["Lb"].next()
    P.cp("scalar", Ld4[:], pd, [bd], [Ld4])
    yield
    pe, be = qd.next()
    P.mm(pe, Ld4[:], R1[:], True, True, [Ld4, R1], [be])
    Dm = W["Rb"].next()
    P.tt("vector", Dm[:], pe, R1[:], ALU.add, [be, R1], [Dm])
    yield
    Tt = None
    for sz in (8, 16, 32, 64):
        pt, bt = qb.next()
        P.tr(pt, Dm[:], identb, [Dm, C.cbf], [bt])
        Dt = W["Dtb"].next()
        P.cp("scalar", Dt[:], pt, [bt], [Dt])
        pE, bE = qd.next()
        P.mm(pE, Lf[:], Dm[:], True, True, [Lf, Dm], [bE])
        E = W["Eb"].next()
        P.tt("vector", E[:], pE, cmat(C, "m%d" % sz), ALU.mult, [bE, C.consts], [E])
        yield
        pF, bF = qd.next()
        P.mm(pF, Dt[:], E[:], True, True, [Dt, E], [bF])
        if sz < 64:
            Dn = W["Rb"].next()
        else:
            Dn = Tt = W["Tt"].next()
        P.tt("vector", Dn[:], Dm[:], pF, ALU.subtract, [Dm, bF], [Dn])
        yield
        Dm = Dn
    kg = W["kg"].next()
    P.act(kg[:], H.ktn[:], AF.Identity, [H.ktn, G.eg], [kg], scale=egc)
    pu, bu = qa.next()
    P.mm(pu, Tt[:], H.vn[hv_i][:], True, True, [Tt, H.vn[hv_i]], [bu])
    u = W["u"].next()
    P.act(u[:], pu, AF.Identity, [bu, G.beta], [u], scale=beta)
    yield
    pw, bw = qa.next()
    P.mm(pw, kg[:], Tt[:], True, True, [kg, Tt], [bw])
    wT = W["wT"].next()
    P.cp("scalar", wT[:], pw, [bw], [wT])
    yield
    pp, bp = qd.next()
    P.mm(pp, wT[:], Sbf[:], True, True, [wT, Sbf], [bp])
    vn = W["vn"].next()
    P.stt(vn[:], pp, nbeta, u[:], ALU.mult, ALU.add, [bp, G.nbeta, u], [vn])
    yield
    po1, bo1 = qd.next()
    P.mm(po1, H.qTn[:], Sbf[:], True, True, [H.qTn, Sbf], [bo1])
    po2, bo2 = qa.next()
    P.mm(po2, AI[:], vn[:], True, True, [AI, vn], [bo2])
    o2 = W["o2"].next()
    P.cp("scalar", o2[:], po2, [bo2], [o2])
    o = W["o"].next()
    P.stt(o[:], po1, egc, o2[:], ALU.mult, ALU.add, [bo1, G.eg, o2], [o])
    P.dma("sync", od[d, h, n * 128:(n + 1) * 128, :], o[:], reads=[o])
    yield
    kd = W["kd"].next()
    P.act(kd[:], H.ktn[:], AF.Identity, [H.ktn, G.eg], [kd], scale=egr)
    pS, bS = qd.next()
    P.mm(pS, kd[:], vn[:], True, True, [kd, vn], [bS])
    P.stt(S[:], S[:], egt, pS, ALU.mult, ALU.add, [S, G.eg, bS], [S])
    P.cp("scalar", Sbf[:], S[:], [S], [Sbf])
    yield


def dn_scan(C, li, ba, qT, kT, ktok, vtok, od):
    P, I = C.P, C.I
    NT, NTC, NTOK = C.NT, C.NTC, C.NTOK
    nheads = C.cfg.get("dn_heads", 16)
    with ExitStack() as s:
        G = Ctx()
        alog = P.sb(s, "alog", [128, 64], F32)
        dtb = P.sb(s, "dtb", [128, 64], F32)
        P.dma("sync", alog[:], I["a_log%d" % li].partition_broadcast(128), writes=[alog])
        P.dma("sync", dtb[:], I["dt_bias%d" % li].partition_broadcast(128), writes=[dtb])
        negA = P.sb(s, "negA", [128, 64], F32)
        P.act(negA[:], alog[:], AF.Exp, [alog], [negA])
        P.ts("vector", negA[:], negA[:], -1.0, None, ALU.mult, None, [negA], [negA])
        G.g_all = P.sb(s, "g_all", [128, NT, 64], F32)
        G.beta = P.sb(s, "beta", [128, NT, 64], F32)
        G.nbeta = P.sb(s, "nbeta", [128, NT, 64], F32)
        G.eg = P.sb(s, "eg", [128, NT, 2, 96], F32)
        gsc = ExitStack()
        ba_sb = P.sb(gsc, "ba", [128, NT, 128], F32)
        P.dma("sync", ba_sb[:], ba.rearrange("(n p) c -> p n c", p=128), writes=[ba_sb])
        xg = P.sb(gsc, "xg", [128, NT, 64], F32)
        t1 = P.sb(gsc, "gt1", [128, NT, 64], F32)
        t2 = P.sb(gsc, "gt2", [128, NT, 64], F32)
        for d in range(2):
            P.act(G.beta[:, :, d * 32:(d + 1) * 32], ba_sb[:, :, d * 64:d * 64 + 32], AF.Sigmoid,
                  [ba_sb], [G.beta])
            for n in range(NT):
                P.tt("gpsimd", xg[:, n, d * 32:(d + 1) * 32], ba_sb[:, n, d * 64 + 32:d * 64 + 64],
                     dtb[:, d * 32:(d + 1) * 32], ALU.add, [ba_sb, dtb], [xg])
        P.ts("gpsimd", G.nbeta[:], G.beta[:], -1.0, None, ALU.mult, None, [G.beta], [G.nbeta])
        P.ts("gpsimd", t2[:], xg[:], -1.0, None, ALU.mult, None, [xg], [t2])
        P.tt("vector", t1[:], xg[:], t2[:], ALU.max, [xg, t2], [t1])
        P.act(t1[:], t1[:], AF.Exp, [t1], [t1], scale=-1.0)
        P.ts("vector", t1[:], t1[:], 1.0, None, ALU.add, None, [t1], [t1])
        P.act(t1[:], t1[:], AF.Ln, [t1], [t1])
        P.stt(t2[:], xg[:], 0.0, t1[:], ALU.max, ALU.add, [xg, t1], [t2])
        for n in range(NT):
            P.tt("gpsimd", G.g_all[:, n, :], t2[:, n, :], negA[:], ALU.mult, [t2, negA], [G.g_all])
        P.barrier()
        gsc.close()
        qa = QPool(P, s, "qpa", 3)
        qd = QPool(P, s, "qpd", 3)
        qb = QPool(P, s, "qpb", 2, dt=BF16)
        for n in range(NT):
            for d in range(2):
                sfx = "F" if d == 0 else "B"
                pg, bg = qa.next()
                rhs = G.g_all[:, n, d * 32:(d + 1) * 32]
                P.mm(pg[:, 0:32], cmat(C, "tri" + sfx), rhs, True, True, [C.consts, G.g_all], [bg])
                P.mm(pg[:, 32:64], cmat(C, "ntri" + sfx), rhs, True, True, [C.consts, G.g_all], [bg])
                P.mm(pg[:, 64:96], cmat(C, "ones"), rhs, True, True, [C.consts, G.g_all], [bg])
                P.act(G.eg[:, n, d, :], pg[:, 0:96], AF.Exp, [bg], [G.eg])
        if "gates" in C.cfg.get("debug", []):
            dump(C, "g_all", G.g_all, [128, NT, 64], F32)
            dump(C, "beta", G.beta, [128, NT, 64], F32)
            dump(C, "eg", G.eg, [128, NT, 2, 96], F32)
        W = {}
        for name, dt, n in [("gS", F32, 12), ("dec", F32, 16), ("Xb", BF16, 32), ("Lb", BF16, 32), ("Rb", BF16, 40),
                            ("u", F32, 16), ("o2", F32, 12), ("o", F32, 12), ("Gm", F32, 8), ("ATm", F32, 8),
                            ("AI", BF16, 16), ("Tt", BF16, 16), ("Lfb", BF16, 12), ("Dtb", BF16, 16), ("Eb", BF16, 16),
                            ("kg", BF16, 16), ("wT", BF16, 16), ("vn", BF16, 16), ("kd", BF16, 16)]:
            W[name] = P.pool(s, "w" + name, [128, 128], dt, n)
        NCH = 12
        kTp = P.pool(s, "kTn", [128, 128], BF16, NCH)
        qTp = P.pool(s, "qTn", [128, 128], BF16, NCH)
        ktp = P.pool(s, "ktn", [128, 128], BF16, NCH)
        vtp = P.pool(s, "vtn", [128, 128], BF16, 2 * NCH)
        HP = 2
        Ss = [P.sb(s, "S", [128, 128], F32) for _ in range(4 * HP)]
        Sb = [P.sb(s, "Sb", [128, 128], BF16) for _ in range(4 * HP)]
        ordF = list(range(NT))
        ordB = list(range(NTC - 1, -1, -1)) + list(range(NT - 1, NTC - 1, -1))
        for hp in range(0, nheads, HP):
            for t_ in Ss + Sb:
                P.op("gpsimd", lambda e, t_=t_: e.memset(t_[:], 0.0), [], [t_])
            for t in range(NT):
                gens = []
                for hi in range(HP):
                    hq = hp + hi
                    for d in range(2):
                        n = (ordF if d == 0 else ordB)[t]
                        sfx = "F" if d == 0 else "B"
                        nsl = slice(n * 128, (n + 1) * 128)
                        H = Ctx()
                        H.kTn, H.qTn, H.ktn = kTp.next(), qTp.next(), ktp.next()
                        H.vn = [vtp.next(), vtp.next()]
                        P.dma("sync", H.kTn[:], kT[hq, :, nsl], writes=[H.kTn])
                        P.dma("sync", H.qTn[:], qT[hq, :, nsl], writes=[H.qTn])
                        P.dma("sync", H.ktn[:], ktok[hq, nsl, :], writes=[H.ktn])
                        for i in range(2):
                            P.dma("sync", H.vn[i][:], vtok[2 * hq + i, nsl, :], writes=[H.vn[i]])
                        pg, bg = qd.next()
                        P.mm(pg, H.kTn[:], H.kTn[:], True, True, [H.kTn], [bg])
                        Gm = W["Gm"].next()
                        P.tt("vector", Gm[:], pg, cmat(C, "mts" + sfx), ALU.mult, [bg, C.consts], [Gm])
                        pa, ba_ = qd.next()
                        P.mm(pa, H.kTn[:], H.qTn[:], True, True, [H.kTn, H.qTn], [ba_])
                        ATm = W["ATm"].next()
                        P.tt("vector", ATm[:], pa, cmat(C, "mti" + sfx), ALU.mult, [ba_, C.consts], [ATm])
                        for hv_i in range(2):
                            ci = hi * 4 + d * 2 + hv_i
                            gens.append(dn_unit(C, W, G, H, qa, qd, qb, hq, hv_i, d, n, Ss[ci], Sb[ci], Gm, ATm, od))
                live = gens
                while live:
                    nxt = []
                    for g in live:
                        try:
                            next(g)
                            nxt.append(g)
                        except StopIteration:
                            pass
                    live = nxt


def dn_epilogue(C, li, od, zs, yT):
    P, I = C.P, C.I
    NT, NTOK = C.NT, C.NTOK
    nhv = 2 * C.cfg.get("dn_heads", 16)
    with ExitStack() as s:
        o0p = P.pool(s, "eo0", [128, NT, 128], F32, 2)
        o1p = P.pool(s, "eo1", [128, NT, 128], F32, 2)
        zp = P.pool(s, "ez", [128, NT, 128], F32, 2)
        ybp = P.pool(s, "eyb", [128, NT, 128], BF16, 2)
        ysp = P.pool(s, "eys", [128, NTOK], BF16, 2)
        msp = P.pool(s, "ems", [128, NT], F32, 2)
        onw = P.sb(s, "onw", [128, 128], F32)
        P.dma("sync", onw[:], I["onorm_w%d" % li].partition_broadcast(128), writes=[onw])
        ps_t = P.pool(s, "pse", [128, 1024], BF16, 2, psum=True)
        identb = cmat(C, "ident", bf=True)
        for hv in range(nhv):
            o0, o1, z, yb, ys, ms = o0p.next(), o1p.next(), zp.next(), ybp.next(), ysp.next(), msp.next()
            P.dma("sync", o0[:], od[0, hv].rearrange("(n p) e -> p n e", p=128), writes=[o0])
            P.dma("sync", o1[:], od[1, hv].rearrange("(n p) e -> p n e", p=128), writes=[o1])
            P.dma("sync", z[:], zs[:, hv * 128:(hv + 1) * 128].rearrange("(n p) e -> p n e", p=128), writes=[z])
            P.tt("gpsimd", o0[:], o0[:], o1[:], ALU.add, [o0, o1], [o0])
            P.tt("gpsimd", o1[:], o0[:], o0[:], ALU.mult, [o0], [o1])
            P.op("vector", lambda e, ms=ms, o1=o1: e.tensor_reduce(out=ms[:], in_=o1[:], axis=AX.X, op=ALU.add),
                 [o1], [ms])
            P.ts("vector", ms[:], ms[:], 1.0 / 128, 1e-6, ALU.mult, ALU.add, [ms], [ms])
            P.act(ms[:], ms[:], AF.Sqrt, [ms], [ms])
            P.op("vector", lambda e, ms=ms: e.reciprocal(out=ms[:], in_=ms[:]), [ms], [ms])
            for n in range(NT):
                P.tt("gpsimd", z[:, n, :], z[:, n, :], onw[:], ALU.mult, [z, onw], [z])
            for n in range(NT):
                P.stt(yb[:, n, :], o0[:, n, :], ms[:, n:n + 1], z[:, n, :], ALU.mult, ALU.mult, [o0, ms, z], [yb])
            for t8 in range(0, NT, 8):
                n8 = min(8, NT - t8)
                ps = ps_t.next()
                for j in range(n8):
                    P.tr(ps[:, j * 128:(j + 1) * 128], yb[:, t8 + j, :], identb, [yb, C.cbf], [ps])
                P.cp("scalar" if (t8 // 8) % 2 == 0 else "vector", ys[:, t8 * 128:(t8 + n8) * 128],
                     ps[:, :n8 * 128], [ps], [ys])
            P.dma("sync", yT[hv * 128:(hv + 1) * 128, :], ys[:], reads=[ys])


def proj_tokmajor(C, w_in, c0, ncols, func, dst, dst_c0, scope_name, out_dt=F32):
    P = C.P
    with ExitStack() as s2:
        wtm = P.pool(s2, scope_name + "w", [128, KD, 512], BF16, 2)
        zst = P.pool(s2, scope_name + "z", [128, 512], out_dt, 3)
        psp = P.pool(s2, scope_name + "p", [128, 512], F32, 4, psum=True)
        for g in range(0, ncols, 512):
            cw = min(512, ncols - g)
            wt = wtm.next()
            P.dma("gpsimd", wt[:, :, :cw], w_in[:, c0 + g:c0 + g + cw].rearrange("(k p) c -> p k c", p=128),
                  writes=[wt])
            for tt in range(C.NT):
                ps = psp.next()
                for k in range(KD):
                    P.mm(ps[:, :cw], C.hT[:, k, tt * 128:(tt + 1) * 128], wt[:, k, :cw], k == 0, k == KD - 1,
                         [wt, C.hT.sub((tt, k))], [ps])
                zt = zst.next()
                if func is None:
                    P.cp("vector", zt[:, :cw], ps[:, :cw], [ps], [zt])
                else:
                    P.act(zt[:, :cw], ps[:, :cw], func, [ps], [zt])
                P.dma("sync", dst[tt * 128:(tt + 1) * 128, dst_c0 + g:dst_c0 + g + cw], zt[:, :cw], reads=[zt])


def tok_tiles(C):
    tiles = []
    for t0 in range(0, C.NCTX, 512):
        tiles.append((t0, min(512, C.NCTX - t0), True))
    for t0 in range(C.NCTX, C.NTOK, 512):
        tiles.append((t0, min(512, C.NTOK - t0), False))
    return tiles


def da_layer(C, li, l, xin, xout):
    import math
    P, I = C.P, C.I
    NTOK, NT, NTC, NCTX, NLAT = C.NTOK, C.NT, C.NTC, C.NCTX, C.NLAT
    w_in = I["w_in%d" % li]
    qT = C.dint("qT%d" % li, [16, 128, NTOK], BF16)
    kT = C.dint("kT%d" % li, [16, 128, NTOK], BF16)
    vtok = C.dint("vtok%d" % li, [NTOK, 2048], BF16)
    gs = C.dint("gs%d" % li, [NTOK, 2048])
    yT = C.dint("yT%d" % li, [2048, NTOK], BF16)
    lambda_init = 0.8 - 0.6 * math.exp(-0.3 * l)
    stop = C.cfg.get("da_stop", 99)
    rope = I["rope%d" % li]
    with ExitStack() as s:
        C.hT = P.sb(s, "hT", [128, KD, NTOK], BF16)
        phase_norm(C, li, xin)
        P.barrier()
        with ExitStack() as s2:
            wfm = P.pool(s2, "wfm", [128, KD, 128], BF16, 3)
            stage = P.pool(s2, "stage", [128, NTOK], BF16, 2)
            csp = P.pool(s2, "cs", [128, 2, 512], F32, 2)
            qbp = P.pool(s2, "qb", [128, 512], BF16, 2)
            t1p = P.pool(s2, "t1", [128, 512], F32, 2)
            t2p = P.pool(s2, "t2", [128, 512], F32, 2)
            psp = P.pool(s2, "pp", [128, 512], F32, 3, psum=True)
            psr = P.pool(s2, "pr", [128, 512], F32, 2, psum=True)
            rotm = cmat(C, "rotm", bf=True)
            for m in range(32):
                wt = wfm.next()
                P.dma("gpsimd", wt[:], w_in[:, m * 128:(m + 1) * 128].rearrange("(k p) c -> p k c", p=128),
                      writes=[wt])
                st = stage.next()
                for (t0, tw, isctx) in tok_tiles(C):
                    ps = psp.next()
                    for k in range(KD):
                        P.mm(ps[:, :tw], wt[:, k, :], C.hT[:, k, t0:t0 + tw], k == 0, k == KD - 1,
                             [wt] + [C.hT.sub((tt, k)) for tt in range(t0 // 128, (t0 + tw) // 128)], [ps])
                    if isctx or "norope" in C.cfg.get("debug", []):
                        P.cp("scalar", st[:, t0:t0 + tw], ps[:, :tw], [ps], [st])
                        continue
                    cs = csp.next()
                    if "r2" not in C.cfg.get("debug", []):
                        P.dma("sync", cs[:, :, :tw], rope[:, :, t0 - NCTX:t0 - NCTX + tw], writes=[cs])
                    qb = qbp.next()
                    P.cp("vector", qb[:, :tw], ps[:, :tw], [ps], [qb])
                    dbgf = C.cfg.get("debug", [])
                    if "r1" in dbgf:
                        pr = ps
                    else:
                        pr = psr.next()
                        P.mm(pr[:, :tw], rotm, qb[:, :tw], True, True, [C.cbf, qb], [pr])
                    t1 = t1p.next()
                    P.tt("vector", t1[:, :tw], ps[:, :tw], cs[:, 0, :tw], ALU.mult, [ps, cs], [t1])
                    t2 = t2p.next()
                    P.tt("vector", t2[:, :tw], pr[:, :tw], cs[:, 1, :tw], ALU.mult, [pr, cs], [t2])
                    P.tt("gpsimd", st[:, t0:t0 + tw], t1[:, :tw], t2[:, :tw], ALU.add, [t1, t2], [st])
                P.dma("sync", (qT if m < 16 else kT)[m % 16, :, :], st[:], reads=[st])
        P.barrier()
        if "nov" not in C.cfg.get("debug", []):
            proj_tokmajor(C, w_in, 4096, 2048, None, vtok, 0, "dav", out_dt=BF16)
        P.barrier()
        if "nog" not in C.cfg.get("debug", []):
            proj_tokmajor(C, w_in, 6144, 2048, AF.Silu, gs, 0, "dag")
    P.barrier()
    if stop <= 1:
        return

    scale = 128.0 ** -0.5
    with ExitStack() as s:
        lam = P.sb(s, "lam", [128, 4, 128], F32)
        P.dma("sync", lam[:], I["lam%d" % li].partition_broadcast(128), writes=[lam])
        lt = P.sb(s, "lamt", [128, 128], F32)
        lsc = P.sb(s, "lamsc", [128, 4], F32)
        for i in range(2):
            P.tt("vector", lt[:], lam[:, 2 * i, :], lam[:, 2 * i + 1, :], ALU.mult, [lam], [lt])
            P.op("vector", lambda e, i=i: e.tensor_reduce(out=lsc[:, i:i + 1], in_=lt[:], axis=AX.X, op=ALU.add),
                 [lt], [lsc])
        P.act(lsc[:, 0:2], lsc[:, 0:2], AF.Exp, [lsc], [lsc])
        P.tt("vector", lsc[:, 2:3], lsc[:, 1:2], lsc[:, 0:1], ALU.subtract, [lsc], [lsc])
        P.ts("vector", lsc[:, 3:4], lsc[:, 2:3], -lambda_init, None, ALU.add, None, [lsc], [lsc])
        neglam = lsc[:, 3:4]
        subw = P.sb(s, "subw", [128, 256], F32)
        P.dma("sync", subw[:], I["subln%d" % li].partition_broadcast(128), writes=[subw])
        P.ts("vector", subw[:], subw[:], 1.0 - lambda_init, None, ALU.mult, None, [subw], [subw])
        qp_ = P.pool(s, "aq", [128, NTOK], BF16, 4)
        kp_ = P.pool(s, "ak", [128, NTOK], BF16, 4)
        vp_ = P.pool(s, "av", [128, NT, 257], BF16, 2)
        for t in vp_.tiles:
            P.op("gpsimd", lambda e, t=t: e.memset(t[:, :, 256:257], 1.0), [], [t])
        ptp = P.pool(s, "apt", [128, 512], BF16, 3)
        ojp = P.pool(s, "aoj", [128, 256], F32, 4)
        op_ = P.pool(s, "ao", [128, 256], F32, 2)
        gp_ = P.pool(s, "ag", [128, 256], F32, 2)
        ybp = P.pool(s, "ayb", [128, 256], BF16, 2)
        ysp = P.pool(s, "ays", [128, 2, NTOK], BF16, 2)
        stp = P.pool(s, "ast", [128, 4], F32, 4)
        sqj = P.sb(s, "asq", [128, 256], BF16)
        pss = P.pool(s, "aps", [128, 512], F32, 2, psum=True)
        psa = [P.ps(s, "apa", [128, 512], F32) for _ in range(4)]
        pst = P.pool(s, "apt2", [128, 1024], BF16, 2, psum=True)
        identb = cmat(C, "ident", bf=True)
        for h in range(C.cfg.get("da_heads", 8)):
            qh = [qp_.next(), qp_.next()]
            kh = [kp_.next(), kp_.next()]
            vh = vp_.next()
            ys = ysp.next()
            for j in range(2):
                P.dma("sync", qh[j][:], qT[2 * h + j, :, :], writes=[qh[j]])
                P.dma("sync", kh[j][:], kT[2 * h + j, :, :], writes=[kh[j]])
            P.dma("sync", vh[:, :, 0:256], vtok[:, h * 256:(h + 1) * 256].rearrange("(n p) e -> p n e", p=128),
                  writes=[vh])
            qtiles = [(0, NCTX, 0, NTC)] + [(q0, min(512, NTOK - q0), 0, NT) for q0 in range(NCTX, NTOK, 512)]
            for (q0, qw, kb0, kb1) in qtiles:
                nsub = qw // 128
                ojs = []
                for j in range(2):
                    def score(kb, j=j, q0=q0, qw=qw):
                        ps = pss.next()
                        P.mm(ps[:, :qw], kh[j][:, kb * 128:(kb + 1) * 128], qh[j][:, q0:q0 + qw], True, True,
                             [kh[j], qh[j]], [ps])
                        return ps
                    ps_next = score(kb0)
                    for kb in range(kb0, kb1):
                        ps = ps_next
                        if kb + 1 < kb1:
                            ps_next = score(kb + 1)
                        pt = ptp.next()
                        P.act(pt[:, :qw], ps[:, :qw], AF.Exp, [ps], [pt], scale=scale)
                        for qs in range(nsub):
                            P.mm(psa[qs][:, 0:257], pt[:, qs * 128:(qs + 1) * 128], vh[:, kb, :], kb == kb0,
                                 kb == kb1 - 1, [pt, vh], [psa[qs]])
                    row = []
                    for qs in range(nsub):
                        st = stp.next()
                        P.op("vector", lambda e, st=st, qs=qs: e.reciprocal(out=st[:, 0:1], in_=psa[qs][:, 256:257]),
                             [psa[qs]], [st])
                        oj = ojp.next() if j == 0 else None
                        if j == 0:
                            P.ts("vector", oj[:], psa[qs][:, 0:256], st[:, 0:1], None, ALU.mult, None,
                                 [psa[qs], st], [oj])
                            row.append(oj)
                        else:
                            o1 = gp_.next()
                            P.ts("vector", o1[:], psa[qs][:, 0:256], st[:, 0:1], None, ALU.mult, None,
                                 [psa[qs], st], [o1])
                            o = op_.next()
                            P.stt(o[:], o1[:], neglam, ojs[qs][:], ALU.mult, ALU.add, [o1, lsc, ojs[qs]], [o])
                            P.act(sqj[:], o[:], AF.Square, [o], [sqj, st], accum_out=st[:, 1:2])
                            P.ts("vector", st[:, 2:3], st[:, 1:2], 1.0 / 256, 1e-5, ALU.mult, ALU.add, [st], [st])
                            P.act(st[:, 2:3], st[:, 2:3], AF.Sqrt, [st], [st])
                            P.op("vector", lambda e, st=st: e.reciprocal(out=st[:, 3:4], in_=st[:, 2:3]), [st], [st])
                            g = gp_.next()
                            tok0 = q0 + qs * 128
                            P.dma("sync", g[:], gs[tok0:tok0 + 128, h * 256:(h + 1) * 256], writes=[g])
                            P.tt("gpsimd", g[:], g[:], subw[:], ALU.mult, [g, subw], [g])
                            yb = ybp.next()
                            P.stt(yb[:], o[:], st[:, 3:4], g[:], ALU.mult, ALU.mult, [o, st, g], [yb])
                            pT = pst.next()
                            for c in range(2):
                                P.tr(pT[:, c * 128:(c + 1) * 128], yb[:, c * 128:(c + 1) * 128], identb,
                                     [yb, C.cbf], [pT])
                            P.cp("scalar", ys[:, :, tok0:tok0 + 128],
                                 pT[:, 0:256].rearrange("p (c t) -> p c t", c=2), [pT], [ys])
                    if j == 0:
                        ojs = row
            for c in range(2):
                P.dma("sync", yT[h * 256 + c * 128:h * 256 + (c + 1) * 128, :], ys[:, c, :], reads=[ys])
    P.barrier()
    if stop <= 2:
        return
    phase_outproj(C, li, I["w_out%d" % li], 16, yT, xin, xout)


def na_layer(C, li, l, xin, xout):
    P, I = C.P, C.I
    NTOK, NT, NTC, NCTX, NLAT = C.NTOK, C.NT, C.NTC, C.NCTX, C.NLAT
    w_in = I["w_in%d" % li]
    nab = I["nab%d" % li]
    qT = C.dint("qT%d" % li, [16, 128, NTOK], BF16)
    kT = C.dint("kT%d" % li, [16, 128, NTOK], BF16)
    vtok = C.dint("vtok%d" % li, [NTOK, 2048], BF16)
    gs = C.dint("gs%d" % li, [NTOK, 2048])
    od = C.dint("od%d" % li, [16, NTOK, 128])
    yT = C.dint("yT%d" % li, [2048, NTOK], BF16)
    stop = C.cfg.get("na_stop", 99)
    qscale = 128.0 ** -0.5
    with ExitStack() as s:
        C.hT = P.sb(s, "hT", [128, KD, NTOK], BF16)
        phase_norm(C, li, xin)
        P.barrier()
        with ExitStack() as s2:
            wfm = P.pool(s2, "wfm", [128, KD, 128], BF16, 3)
            stage = P.pool(s2, "stage", [128, NTOK], BF16, 2)
            psp = P.pool(s2, "pp", [128, 512], F32, 4, psum=True)
            ev = 0
            for m in range(32):
                wt = wfm.next()
                P.dma("gpsimd", wt[:], w_in[:, m * 128:(m + 1) * 128].rearrange("(k p) c -> p k c", p=128),
                      writes=[wt])
                st = stage.next()
                for t0 in range(0, NTOK, 512):
                    tw = min(512, NTOK - t0)
                    ps = psp.next()
                    for k in range(KD):
                        P.mm(ps[:, :tw], wt[:, k, :], C.hT[:, k, t0:t0 + tw], k == 0, k == KD - 1,
                             [wt] + [C.hT.sub((tt, k)) for tt in range(t0 // 128, (t0 + tw) // 128)], [ps])
                    sc = qscale if m < 16 else 1.0
                    if ev % 2 == 0:
                        P.act(st[:, t0:t0 + tw], ps[:, :tw], AF.Identity, [ps], [st], scale=sc)
                    else:
                        P.ts("vector", st[:, t0:t0 + tw], ps[:, :tw], sc, None, ALU.mult, None, [ps], [st])
                    ev += 1
                P.dma("sync", (qT if m < 16 else kT)[m % 16, :, :], st[:], reads=[st])
        P.barrier()
        proj_tokmajor(C, w_in, 4096, 2048, None, vtok, 0, "nav", out_dt=BF16)
        P.barrier()
        proj_tokmajor(C, w_in, 6144, 2048, AF.Silu, gs, 0, "nag")
    P.barrier()
    if stop <= 1:
        return
    rows = NLAT // 64
    wr = min(8, rows)
    with ExitStack() as s:
        qp_ = P.pool(s, "nq", [128, NTOK], BF16, 2)
        kp_ = P.pool(s, "nk", [128, NTOK], BF16, 2)
        vp_ = P.pool(s, "nv", [128, NT, 129], BF16, 2)
        for t in vp_.tiles:
            P.op("gpsimd", lambda e, t=t: e.memset(t[:, :, 128:129], 1.0), [], [t])
        nbp = P.pool(s, "nnb", [128, 19, 64], F32, 2)
        sbp = P.pool(s, "nsb", [128, 5, 64], F32, 3)
        ptp = P.pool(s, "npt", [128, 5 * 64 + NTC * 64], BF16, 3)
        ptc = P.pool(s, "nptc", [128, NCTX], BF16, 2)
        oap = P.pool(s, "noa", [64, rows, 128], F32, 2)
        ocp = P.pool(s, "noc", [128, NTC, 128], F32, 2)
        stp = P.pool(s, "nst", [128, 2], F32, 4)
        psw = P.pool(s, "npw", [128, 512], F32, 2, psum=True)
        psc = P.pool(s, "npc", [128, 512], F32, 2, psum=True)
        psa = P.pool(s, "npa", [128, 512], F32, 2, psum=True)
        for h in range(C.cfg.get("na_heads", 16)):
            qh, kh, vh, nb = qp_.next(), kp_.next(), vp_.next(), nbp.next()
            P.dma("sync", qh[:], qT[h, :, :], writes=[qh])
            P.dma("sync", kh[:], kT[h, :, :], writes=[kh])
            P.dma("sync", vh[:, :, 0:128], vtok[:, h * 128:(h + 1) * 128].rearrange("(n p) e -> p n e", p=128),
                  writes=[vh])
            P.dma("sync", nb[:], nab[h], writes=[nb])
            oa = oap.next()
            def stage1(r):
                rs = min(max(r - 4, 0), rows - wr)
                if rs % 2 == 0:
                    npair = wr // 2
                    pairs = [rs // 2 + i for i in range(npair)]
                    dr0 = rs - r + 7
                    btiles = [dr0 + 2 * i for i in range(npair)]
                else:
                    npair = 5
                    pairs = [(rs - 1) // 2 + i for i in range(5)]
                    btiles = [14 + i for i in range(5)]
                q_sl = qh[:, NCTX + r * 64:NCTX + (r + 1) * 64]
                pw = psw.next()
                for i, m in enumerate(pairs):
                    tt = NTC + m
                    P.mm(pw[:, i * 64:(i + 1) * 64], kh[:, tt * 128:(tt + 1) * 128], q_sl, True, True,
                         [kh, qh], [pw])
                pc = psc.next()
                for c in range(NTC):
                    P.mm(pc[:, c * 64:(c + 1) * 64], kh[:, c * 128:(c + 1) * 128], q_sl, True, True,
                         [kh, qh], [pc])
                sb = sbp.next()
                if btiles[-1] - btiles[0] == npair - 1:
                    P.tt("vector", sb[:, 0:npair, :], pw[:, 0:npair * 64].rearrange("p (i q) -> p i q", q=64),
                         nb[:, btiles[0]:btiles[0] + npair, :], ALU.add, [pw, nb], [sb])
                else:
                    for i in range(npair):
                        P.tt("vector", sb[:, i, :], pw[:, i * 64:(i + 1) * 64], nb[:, btiles[i], :], ALU.add,
                             [pw, nb], [sb])
                pt = ptp.next()
                P.act(pt[:, 0:npair * 64], sb[:, 0:npair, :], AF.Exp, [sb], [pt])
                P.act(pt[:, npair * 64:(npair + NTC) * 64], pc[:, 0:NTC * 64], AF.Exp, [pc], [pt])
                return (r, pt, pairs, npair)

            def stage2(r, pt, pairs, npair):
                acc = psa.next()
                for i, m in enumerate(pairs):
                    P.mm(acc[0:64, 0:129], pt[:, i * 64:(i + 1) * 64], vh[:, NTC + m, :], i == 0, False,
                         [pt, vh], [acc])
                for c in range(NTC):
                    P.mm(acc[0:64, 0:129], pt[:, (npair + c) * 64:(npair + c + 1) * 64], vh[:, c, :], False,
                         c == NTC - 1, [pt, vh], [acc])
                st = stp.next()
                P.op("vector", lambda e, st=st, acc=acc: e.reciprocal(out=st[0:64, 0:1], in_=acc[0:64, 128:129]),
                     [acc], [st])
                P.ts("vector", oa[0:64, r, :], acc[0:64, 0:128], st[0:64, 0:1], None, ALU.mult, None,
                     [acc, st], [oa])

            prev = None
            for r in range(rows):
                cur = stage1(r)
                if prev is not None:
                    stage2(*prev)
                prev = cur
            stage2(*prev)
            P.dma("sync", od[h, NCTX:NTOK, :].rearrange("(r p) e -> p r e", p=64), oa[:], reads=[oa])
            oc = ocp.next()
            accs = [psa.next() for _ in range(NTC)]
            for kb in range(NTC):
                pc = psc.next()
                P.mm(pc[:, 0:NCTX], kh[:, kb * 128:(kb + 1) * 128], qh[:, 0:NCTX], True, True, [kh, qh], [pc])
                pt2 = ptc.next()
                P.act(pt2[:, 0:NCTX], pc[:, 0:NCTX], AF.Exp, [pc], [pt2])
                for qs in range(NTC):
                    P.mm(accs[qs][:, 0:129], pt2[:, qs * 128:(qs + 1) * 128], vh[:, kb, :], kb == 0, kb == NTC - 1,
                         [pt2, vh], [accs[qs]])
            for qs in range(NTC):
                st = stp.next()
                P.op("vector", lambda e, st=st, a=accs[qs]: e.reciprocal(out=st[:, 0:1], in_=a[:, 128:129]),
                     [accs[qs]], [st])
                P.ts("vector", oc[:, qs, :], accs[qs][:, 0:128], st[:, 0:1], None, ALU.mult, None,
                     [accs[qs], st], [oc])
            P.dma("sync", od[h, 0:NCTX, :].rearrange("(n p) e -> p n e", p=128), oc[:], reads=[oc])
    P.barrier()
    if stop <= 2:
        return
    with ExitStack() as s:
        op_ = P.pool(s, "eo", [128, NT, 128], F32, 2)
        gp_ = P.pool(s, "eg", [128, NT, 128], F32, 2)
        ybp = P.pool(s, "eyb", [128, NT, 128], BF16, 2)
        ysp = P.pool(s, "eys", [128, NTOK], BF16, 2)
        ps_t = P.pool(s, "pse", [128, 1024], BF16, 2, psum=True)
        identb = cmat(C, "ident", bf=True)
        for h in range(16):
            o, g, yb, ys = op_.next(), gp_.next(), ybp.next(), ysp.next()
            P.dma("sync", o[:], od[h].rearrange("(n p) e -> p n e", p=128), writes=[o])
            P.dma("sync", g[:], gs[:, h * 128:(h + 1) * 128].rearrange("(n p) e -> p n e", p=128), writes=[g])
            P.tt("gpsimd" if h % 2 == 0 else "vector", yb[:], o[:], g[:], ALU.mult, [o, g], [yb])
            for t8 in range(0, NT, 8):
                n8 = min(8, NT - t8)
                ps = ps_t.next()
                for j in range(n8):
                    P.tr(ps[:, j * 128:(j + 1) * 128], yb[:, t8 + j, :], identb, [yb, C.cbf], [ps])
                P.cp("scalar" if (t8 // 8) % 2 == 0 else "vector", ys[:, t8 * 128:(t8 + n8) * 128],
                     ps[:, :n8 * 128], [ps], [ys])
            P.dma("sync", yT[h * 128:(h + 1) * 128, :], ys[:], reads=[ys])
    P.barrier()
    if stop <= 3:
        return
    phase_outproj(C, li, I["w_out%d" % li], 16, yT, xin, xout)
```

```python
import numpy as np
from contextlib import ExitStack
import concourse.bass as bass
import concourse.mybir as mybir
from concourse.bass_utils import run_bass_kernel_spmd

F32 = mybir.dt.float32
BF16 = mybir.dt.bfloat16
AF = mybir.ActivationFunctionType
ALU = mybir.AluOpType
AX = mybir.AxisListType

D = 2048
KD = 16
ENGS = ["tensor", "vector", "scalar", "gpsimd", "sync"]
GEN = 30000
NDSEM = 24


class Buf:
    __slots__ = ("w", "r")

    def __init__(self):
        self.w = None
        self.r = []


class T:
    def __init__(self, t):
        self.t = t
        self.b = Buf()
        self.subs = {}

    def __getitem__(self, idx):
        return self.t[idx]

    def sub(self, key):
        s = self.subs.get(key)
        if s is None:
            s = self.subs[key] = Buf()
        return s


def _b(x):
    return x.b if isinstance(x, T) else x


class Pool:
    def __init__(self, tiles):
        self.tiles = tiles
        self.i = 0

    def next(self):
        t = self.tiles[self.i]
        self.i = (self.i + 1) % len(self.tiles)
        return t


class Prog:
    def __init__(self, nc, es):
        self.nc = nc
        self.es = es
        self.q = {e: [] for e in ENGS}
        self.cnt = {e: 0 for e in ENGS}
        self.esem = {e: [] for e in ENGS}
        self.known = {e: {} for e in ENGS}
        self.events = {e: [] for e in ENGS}
        self.ptr = {e: {f: 0 for f in ENGS} for e in ENGS}
        self.dsem = [es.enter_context(nc.semaphore("dq%d" % i)) for i in range(NDSEM)]
        self.dtot = [0] * NDSEM
        self.dnext = {"sync": 0, "gpsimd": 0, "scalar": 0}
        self.drange = {"sync": (0, NDSEM // 2), "scalar": (0, NDSEM // 2), "gpsimd": (NDSEM // 2, NDSEM)}
        self.nuid = 0

    def uid(self, base):
        self.nuid += 1
        return "%s_%d" % (base, self.nuid)

    def sb(self, scope, name, shape, dt):
        return T(scope.enter_context(self.nc.sbuf_tensor(self.uid(name), list(shape), dt)))

    def ps(self, scope, name, shape, dt):
        return T(scope.enter_context(self.nc.psum_tensor(self.uid(name), list(shape), dt)))

    def pool(self, scope, name, shape, dt, n, psum=False):
        f = self.ps if psum else self.sb
        return Pool([f(scope, name, shape, dt) for _ in range(n)])

    def _sem_for(self, eng, seq):
        g = (seq - 1) // GEN
        while len(self.esem[eng]) <= g:
            self.esem[eng].append(
                self.es.enter_context(self.nc.semaphore("e_%s_%d" % (eng, len(self.esem[eng])))))
        return self.esem[eng][g], seq - g * GEN

    def _learn(self, eng, key, val):
        kn = self.known[eng]
        if kn.get(key, 0) < val:
            kn[key] = val
            self.events[eng].append((self.cnt[eng] + 1, key, val))

    def _merge_from(self, eng, f, v):
        ev = self.events[f]
        i = self.ptr[eng][f]
        n = len(ev)
        while i < n and ev[i][0] <= v:
            _, key, val = ev[i]
            if not (key[0] == "e" and key[1] == eng):
                self._learn(eng, key, val)
            i += 1
        self.ptr[eng][f] = i

    def _waits(self, eng, deps):
        out = []
        kn = self.known[eng]
        best = {}
        for tok in deps:
            key = tok[:2]
            if tok[2] > best.get(key, 0):
                best[key] = tok[2]
        for key, val in sorted(best.items(), key=lambda kv: (kv[0][0] != "e", -kv[1])):
            if kn.get(key, 0) >= val:
                continue
            if key[0] == "e":
                if key[1] == eng and eng == "tensor":
                    continue
                sem, v = self._sem_for(key[1], val)
                out.append((sem, v))
                self._learn(eng, key, val)
                if key[1] != eng:
                    self._merge_from(eng, key[1], val)
            else:
                out.append((self.dsem[key[1]], val))
                self._learn(eng, key, val)
        return out

    def _collect(self, reads, writes, same_eng=None):
        deps = set()
        for b in reads:
            b = _b(b)
            if b.w is not None:
                deps.add(b.w)
        for b in writes:
            b = _b(b)
            if b.w is not None:
                deps.add(b.w)
            for r in b.r:
                deps.add(r)
        return deps

    def _record(self, tok, reads, writes):
        for b in reads:
            b = _b(b)
            b.r = [r for r in b.r if r[:2] != tok[:2]] + [tok]
        for b in writes:
            b = _b(b)
            b.w = tok
            b.r = []

    def op(self, eng, fn, reads=(), writes=()):
        deps = self._collect(reads, writes, same_eng=eng)
        waits = self._waits(eng, deps)
        seq = self.cnt[eng] + 1
        self.cnt[eng] = seq
        sem, _ = self._sem_for(eng, seq)
        self.q[eng].append((waits, fn, sem, 1))
        self._record(("e", eng, seq), reads, writes)

    def dma(self, eng, out, in_, reads=(), writes=(), **kw):
        lo, hi = self.drange[eng]
        qk = "gpsimd" if eng == "gpsimd" else "sync"
        idx = lo + self.dnext[qk]
        self.dnext[qk] = (self.dnext[qk] + 1) % (hi - lo)
        deps = self._collect(reads, writes)
        if self.dtot[idx] > 0:
            deps.add(("d", idx, self.dtot[idx]))
        waits = self._waits(eng, deps)
        self.dtot[idx] += 16
        tok = ("d", idx, self.dtot[idx])
        self.q[eng].append((waits, (lambda e: e.dma_start(out=out, in_=in_, **kw)), self.dsem[idx], 16))
        self._record(tok, reads, writes)

    def barrier(self):
        deps = set()
        for e in ENGS:
            if self.cnt[e] > 0:
                deps.add(("e", e, self.cnt[e]))
        for i in range(NDSEM):
            if self.dtot[i] > 0:
                deps.add(("d", i, self.dtot[i]))
        for e in ENGS:
            d2 = set(t for t in deps if not (t[0] == "e" and t[1] == e))
            waits = self._waits(e, d2)
            if waits:
                self.q[e].append((waits, None, None, 0))

    def emit(self, blk):
        for eng in ENGS:
            items = self.q[eng]
            if not items:
                continue

            def body(e, items=items):
                for waits, fn, sem, inc in items:
                    for s, v in waits:
                        e.wait_ge(s, v)
                    if fn is not None:
                        fn(e).then_inc(sem, inc)

            getattr(blk, eng)(body)

    def mm(self, out, lhsT, rhs, start, stop, reads, writes):
        self.op("tensor", lambda e: e.matmul(out, lhsT=lhsT, rhs=rhs, start=start, stop=stop),
                reads, writes)

    def tr(self, out, in_, ident, reads, writes):
        self.op("tensor", lambda e: e.transpose(out=out, in_=in_, identity=ident), reads, writes)

    def act(self, out, in_, func, reads, writes, eng="scalar", **kw):
        self.op(eng, lambda e: e.activation(out=out, in_=in_, func=func, **kw), reads, writes)

    def ts(self, eng, out, in0, s1, s2, op0, op1, reads, writes):
        if op1 is None:
            self.op(eng, lambda e: e.tensor_scalar(out=out, in0=in0, scalar1=s1, scalar2=None, op0=op0),
                    reads, writes)
        else:
            self.op(eng, lambda e: e.tensor_scalar(out=out, in0=in0, scalar1=s1, scalar2=s2, op0=op0, op1=op1),
                    reads, writes)

    def tt(self, eng, out, in0, in1, op, reads, writes):
        self.op(eng, lambda e: e.tensor_tensor(out=out, in0=in0, in1=in1, op=op), reads, writes)

    def stt(self, out, in0, scalar, in1, op0, op1, reads, writes):
        self.op("vector", lambda e: e.scalar_tensor_tensor(out=out, in0=in0, scalar=scalar, in1=in1,
                                                           op0=op0, op1=op1), reads, writes)

    def cp(self, eng, out, in_, reads, writes):
        if eng == "scalar":
            self.op(eng, lambda e: e.activation(out=out, in_=in_, func=AF.Copy), reads, writes)
        else:
            self.op(eng, lambda e: e.tensor_copy(out=out, in_=in_), reads, writes)


CONST_NAMES = ["ident", "ones", "triF", "triB", "ntriF", "ntriB", "strF", "strB",
               "mtsF", "mtsB", "mtiF", "mtiB", "m0", "m8", "m16", "m32", "m64", "rotm"]


def make_consts():
    c = np.arange(128)[:, None]
    i = np.arange(128)[None, :]
    m = {}
    m["ident"] = (c == i)
    m["ones"] = np.ones((128, 128), bool)
    m["triF"] = (c <= i)
    m["triB"] = (c >= i)
    m["ntriF"] = ~(c <= i)
    m["ntriB"] = ~(c >= i)
    m["strF"] = (c > i)
    m["strB"] = (c < i)
    m["mtsF"] = (i > c)
    m["mtsB"] = (i < c)
    m["mtiF"] = (i >= c)
    m["mtiB"] = (i <= c)
    m["m0"] = (c // 8 == i // 8)
    for sz in (8, 16, 32, 64):
        m["m%d" % sz] = (c // (2 * sz) == i // (2 * sz)) & (c // sz != i // sz)
    rot = np.zeros((128, 128), np.float32)
    for d in range(128):
        if (d // 32) % 2 == 0:
            rot[d + 32, d] = -1.0
        else:
            rot[d - 32, d] = 1.0
    m["rotm"] = rot
    return np.concatenate([m[n].astype(np.float32) for n in CONST_NAMES], axis=1)


class Ctx:
    pass


def build(cfg):
    NCTX, NLAT = cfg["NCTX"], cfg["NLAT"]
    NTOK = NCTX + NLAT
    NT = NTOK // 128
    NTC = NCTX // 128
    layers = cfg["layers"]
    NL = max(len(layers), 1)
    dbg = cfg.get("debug", [])
    nc = bass.Bass("TRN2", target_bir_lowering=False)
    C = Ctx()
    C.cfg, C.NCTX, C.NLAT, C.NTOK, C.NT, C.NTC = cfg, NCTX, NLAT, NTOK, NT, NTC
    C.nc = nc

    def din(name, shape, dt=F32):
        return nc.dram_tensor(name, list(shape), dt, kind="ExternalInput").ap()

    def dint(name, shape, dt=F32):
        kind = "ExternalOutput" if name in cfg.get("debug_out", []) else "Internal"
        return nc.dram_tensor(name, list(shape), dt, kind=kind).ap()

    def dout(name, shape, dt=F32):
        return nc.dram_tensor(name, list(shape), dt, kind="ExternalOutput").ap()

    C.dint, C.dout = dint, dout
    I = C.I = {}
    I["xs"] = din("xs", [NTOK, D])
    I["cvec"] = din("cvec", [2, D])
    I["consts"] = din("consts", [128, 128 * len(CONST_NAMES)])
    I["norm_w"] = din("norm_w", [NL, D])
    I["ada_w"] = din("ada_w", [NL, D, 3 * D])
    I["ada_b"] = din("ada_b", [NL, 3 * D])
    I["final_norm_w"] = din("final_norm_w", [D])
    for li, l in enumerate(layers):
        kind = l % 3
        if kind == 0:
            I["w_in%d" % li] = din("w_in%d" % li, [D, 12416])
            I["conv_w%d" % li] = din("conv_w%d" % li, [8192, 5])
            I["a_log%d" % li] = din("a_log%d" % li, [64])
            I["dt_bias%d" % li] = din("dt_bias%d" % li, [64])
            I["onorm_w%d" % li] = din("onorm_w%d" % li, [128])
            I["w_out%d" % li] = din("w_out%d" % li, [4096, D])
        elif kind == 1:
            I["w_in%d" % li] = din("w_in%d" % li, [D, 8192])
            I["lam%d" % li] = din("lam%d" % li, [512])
            I["subln%d" % li] = din("subln%d" % li, [256])
            I["w_out%d" % li] = din("w_out%d" % li, [2048, D])
            I["rope%d" % li] = din("rope%d" % li, [128, 2, NLAT])
        else:
            I["w_in%d" % li] = din("w_in%d" % li, [D, 8192])
            I["nab%d" % li] = din("nab%d" % li, [16, 128, 19, 64])
            I["w_out%d" % li] = din("w_out%d" % li, [2048, D])
    C.out = dout("out", [NLAT, D])
    C.xa = dint("xa", [NTOK, D])
    C.xb = dint("xb", [NTOK, D])
    C.gate_row = dint("gate_row", [NL, 2, D])
    C.dbg = {}

    with ExitStack() as es:
        P = Prog(nc, es)
        C.P = P
        C.consts = P.sb(es, "consts", [128, 128 * len(CONST_NAMES)], F32)
        C.cbf = P.sb(es, "cbf", [128, 128 * len(CONST_NAMES)], BF16)
        C.gainT = P.sb(es, "gainT", [128, NL, KD, 2], F32)
        C.shiftT = P.sb(es, "shiftT", [128, NL, KD, 2], F32)
        P.dma("sync", C.consts[:], I["consts"][:, :], writes=[C.consts])
        P.cp("vector", C.cbf[:], C.consts[:], [C.consts], [C.cbf])

        phase_mod(C)
        P.barrier()
        xin = I["xs"]
        xouts = [C.xa, C.xb]
        for li, l in enumerate(layers):
            xout = xouts[li % 2]
            kind = l % 3
            if kind == 0:
                dn_layer(C, li, l, xin, xout)
            elif kind == 1:
                da_layer(C, li, l, xin, xout)
            else:
                na_layer(C, li, l, xin, xout)
            P.barrier()
            xin = xout
        if cfg.get("final", True):
            phase_final(C, xin)
        P.barrier()
        blk = es.enter_context(nc.Block())
        P.emit(blk)
    return nc


def cmat(C, name, bf=False):
    i = CONST_NAMES.index(name)
    t = C.cbf if bf else C.consts
    return t[:, i * 128:(i + 1) * 128]


def phase_mod(C):
    P, I = C.P, C.I
    NL = len(C.cfg["layers"])
    if NL == 0:
        return
    with ExitStack() as s:
        cT = P.sb(s, "cT", [128, KD, 2], F32)
        scT = P.sb(s, "scT", [128, KD, 2], F32)
        for w in range(2):
            src = I["cvec"][w, :].rearrange("(k p o) -> p k o", p=128, o=1)
            P.dma("sync", cT[:, :, w:w + 1], src, writes=[cT], allow_slow_non_contiguous=True)
        P.act(scT[:], cT[:], AF.Silu, [cT], [scT])
        wpool = P.pool(s, "adaw", [128, KD, 512], F32, 2)
        pmod = P.ps(s, "pmod", [128, 96], F32)
        modsb = P.sb(s, "modsb", [128, 48, 2], F32)
        abT = P.sb(s, "abT", [128, 48], F32)
        nwT = P.sb(s, "nwT", [128, KD], F32)
        for li in range(NL):
            P.dma("sync", abT[:].unsqueeze(2) if False else abT[:, :],
                  I["ada_b"][li, :].rearrange("(m p) -> p m", p=128), writes=[abT],
                  allow_slow_non_contiguous=True)
            P.dma("sync", nwT[:, :], I["norm_w"][li, :].rearrange("(k p) -> p k", p=128), writes=[nwT],
                  allow_slow_non_contiguous=True)
            for cg in range(12):
                wg = wpool.next()
                P.dma("sync", wg[:], I["ada_w"][li, :, cg * 512:(cg + 1) * 512].rearrange(
                    "(k p) c -> p k c", p=128), writes=[wg])
                for m in range(4):
                    r0 = (cg * 4 + m) * 2
                    for k in range(KD):
                        P.mm(pmod[:, r0:r0 + 2], wg[:, k, m * 128:(m + 1) * 128], scT[:, k, :],
                             k == 0, k == KD - 1, [wg, scT], [pmod])
            pm = pmod[:, :].rearrange("p (m w) -> p m w", w=2)
            for w in range(2):
                P.tt("vector", modsb[:, :, w], pm[:, :, w], abT[:, :], ALU.add, [pmod, abT], [modsb])
            for w in range(2):
                P.stt(C.gainT[:, li, :, w], modsb[:, 16:32, w], 1.0, nwT[:, :], ALU.add, ALU.mult,
                      [modsb, nwT], [C.gainT])
                P.cp("vector", C.shiftT[:, li, :, w], modsb[:, 0:16, w], [modsb], [C.shiftT])
                P.dma("sync", C.gate_row[li, w, :].rearrange("(k p o) -> p k o", p=128, o=1),
                      modsb[:, 32:48, w:w + 1], reads=[modsb], allow_slow_non_contiguous=True)


def phase_norm(C, li, xin):
    P = C.P
    with ExitStack() as s:
        xpool = P.pool(s, "xt", [128, D], F32, 2)
        xnpool = P.pool(s, "xn", [128, D], F32, 2)
        sqj = P.sb(s, "sqj", [128, D], BF16)
        stat = P.pool(s, "stat", [128, 4], F32, 2)
        pspool = P.pool(s, "pst", [128, 512], F32, 4, psum=True)
        ident = cmat(C, "ident")
        for tt in range(C.NT):
            w = 1 if tt < C.NTC else 0
            xt = xpool.next()
            xn = xnpool.next()
            st = stat.next()
            P.dma("sync", xt[:], xin[tt * 128:(tt + 1) * 128, :], writes=[xt])
            P.act(sqj[:], xt[:], AF.Square, [xt], [sqj, st], accum_out=st[:, 0:1])
            P.ts("vector", st[:, 1:2], st[:, 0:1], 1.0 / D, 1e-6, ALU.mult, ALU.add, [st], [st])
            P.act(st[:, 2:3], st[:, 1:2], AF.Sqrt, [st], [st])
            P.op("vector", lambda e, st=st: e.reciprocal(out=st[:, 3:4], in_=st[:, 2:3]), [st], [st])
            P.act(xn[:], xt[:], AF.Identity, [xt, st], [xn], scale=st[:, 3:4])
            stop = C.cfg.get("norm_stop", 9)
            if stop <= 1:
                continue
            for k4 in range(4):
                ps = pspool.next()
                for j in range(4):
                    k = k4 * 4 + j
                    P.tr(ps[:, j * 128:(j + 1) * 128], xn[:, k * 128:(k + 1) * 128], ident,
                         [xn, C.consts], [ps])
                for j in range(4):
                    if stop <= 2:
                        continue
                    if stop <= 3 and j % 2 == 1:
                        continue
                    k = k4 * 4 + j
                    dst = C.hT[:, k, tt * 128:(tt + 1) * 128]
                    hb = C.hT.sub((tt, k))
                    g = C.gainT[:, li, k, w:w + 1]
                    sh = C.shiftT[:, li, k, w:w + 1]
                    P.ts("vector", dst, ps[:, j * 128:(j + 1) * 128], g, sh, ALU.mult, ALU.add,
                         [ps, C.gainT, C.shiftT], [hb])


def hT_bufs(C, t0, t1):
    return [C.hT.sub((tt, k)) for tt in range(t0 // 128, (t1 + 127) // 128) for k in range(KD)]


def phase_final(C, xin):
    P, I = C.P, C.I
    with ExitStack() as s:
        xpool = P.pool(s, "fx", [128, D], F32, 2)
        opool = P.pool(s, "fo", [128, D], F32, 2)
        sqj = P.sb(s, "fsq", [128, D], BF16)
        stat = P.pool(s, "fstat", [128, 4], F32, 2)
        fw = P.sb(s, "fw", [128, D], F32)
        P.dma("sync", fw[:], I["final_norm_w"].partition_broadcast(128), writes=[fw])
        for tt in range(C.NTC, C.NT):
            xt = xpool.next()
            ot = opool.next()
            st = stat.next()
            P.dma("sync", xt[:], xin[tt * 128:(tt + 1) * 128, :], writes=[xt])
            P.act(sqj[:], xt[:], AF.Square, [xt], [sqj, st], accum_out=st[:, 0:1])
            P.ts("vector", st[:, 1:2], st[:, 0:1], 1.0 / D, 1e-6, ALU.mult, ALU.add, [st], [st])
            P.act(st[:, 2:3], st[:, 1:2], AF.Sqrt, [st], [st])
            P.op("vector", lambda e, st=st: e.reciprocal(out=st[:, 3:4], in_=st[:, 2:3]), [st], [st])
            P.stt(ot[:], xt[:], st[:, 3:4], fw[:], ALU.mult, ALU.mult, [xt, st, fw], [ot])
            r0 = (tt - C.NTC) * 128
            P.dma("sync", C.out[r0:r0 + 128, :], ot[:], reads=[ot])


def dump(C, name, tile, shape, dt):
    ap = C.dout("dbg_" + name, shape, dt)
    C.P.dma("sync", ap, tile[:], reads=[tile])


def make_in_map(inp, b, cfg):
    layers = cfg["layers"]
    m = {}
    m["xs"] = np.ascontiguousarray(np.concatenate([inp["ctx"][b], inp["x"][b]], axis=0), dtype=np.float32)
    m["cvec"] = np.ascontiguousarray(np.stack([inp["c"][b], inp["c_ctx"]], axis=0), dtype=np.float32)
    m["consts"] = make_consts()
    m["norm_w"] = np.ascontiguousarray(np.stack([inp["norm_w"][l] for l in layers]) if layers else np.zeros((1, D), np.float32))
    m["ada_w"] = np.ascontiguousarray(np.stack([inp["ada_w"][l] for l in layers]) if layers else np.zeros((1, D, 3 * D), np.float32))
    m["ada_b"] = np.ascontiguousarray(np.stack([inp["ada_b"][l] for l in layers]) if layers else np.zeros((1, 3 * D), np.float32))
    m["final_norm_w"] = np.ascontiguousarray(inp["final_norm_w"], dtype=np.float32)
    for li, l in enumerate(layers):
        kind, j = l % 3, l // 3
        if kind == 0:
            m["w_in%d" % li] = np.ascontiguousarray(inp["dn_w_in"][j])
            m["conv_w%d" % li] = np.ascontiguousarray(inp["dn_conv_w"][j])
            m["a_log%d" % li] = np.ascontiguousarray(inp["dn_a_log"][j].reshape(64))
            m["dt_bias%d" % li] = np.ascontiguousarray(inp["dn_dt_bias"][j].reshape(64))
            m["onorm_w%d" % li] = np.ascontiguousarray(inp["dn_onorm_w"][j])
            m["w_out%d" % li] = np.ascontiguousarray(inp["dn_w_out"][j])
        elif kind == 1:
            m["w_in%d" % li] = np.ascontiguousarray(inp["da_w_in"][j])
            m["lam%d" % li] = np.ascontiguousarray(inp["da_lambda"][j].reshape(512))
            m["subln%d" % li] = np.ascontiguousarray(inp["da_subln_w"][j])
            m["w_out%d" % li] = np.ascontiguousarray(inp["da_w_out"][j])
            m["rope%d" % li] = rope_table(cfg["NLAT"])
        else:
            m["w_in%d" % li] = np.ascontiguousarray(inp["na_w_in"][j])
            m["nab%d" % li] = na_bias_table(inp["na_rpb"][j])
            m["w_out%d" % li] = np.ascontiguousarray(inp["na_w_out"][j])
    return m


NEG = -30000.0


def rope_table(nlat):
    t = np.arange(nlat)
    row = (t // 64).astype(np.float32)
    col = (t % 64).astype(np.float32)
    inv = (np.float32(10000.0) ** (-np.arange(0, 64, 2, dtype=np.float32) / np.float32(64))).astype(np.float32)
    ang_r = row[:, None] * inv
    ang_c = col[:, None] * inv
    ang = np.concatenate([ang_r, ang_r, ang_c, ang_c], axis=-1)
    return np.ascontiguousarray(np.stack([np.cos(ang).T, np.sin(ang).T], axis=1).astype(np.float32))


def na_bias_table(rpb):
    H = rpb.shape[0]
    kc = np.arange(64)[:, None]
    qc = np.arange(64)[None, :]
    cs = np.clip(qc - 8, 0, 48)
    valid = (kc >= cs) & (kc < cs + 16)
    dc = np.clip(kc - qc + 15, 0, 30)
    B1 = np.where(valid[None, None], rpb[:, :, dc], np.float32(NEG)).astype(np.float32)
    negt = np.full((H, 64, 64), NEG, np.float32)
    tiles = []
    for dr in range(14):
        tiles.append(np.concatenate([B1[:, dr], B1[:, dr + 1]], axis=1))
    tiles.append(np.concatenate([negt, B1[:, 3]], axis=1))
    for dr in (4, 6, 8):
        tiles.append(np.concatenate([B1[:, dr], B1[:, dr + 1]], axis=1))
    tiles.append(np.concatenate([B1[:, 10], negt], axis=1))
    return np.ascontiguousarray(np.stack(tiles, axis=2).astype(np.float32))


def kernel(**inp):
    inp = {k: np.asarray(v) for k, v in inp.items()}
    B, NLAT, _ = inp["x"].shape
    NCTX = inp["ctx"].shape[1]
    cfg = dict(NCTX=NCTX, NLAT=NLAT, layers=[0, 1, 2, 3], final=True)
    nc = build(cfg)
    in_maps = [make_in_map(inp, b, cfg) for b in range(B)]
    res = run_bass_kernel_spmd(nc, in_maps, core_ids=list(range(B)))
    return np.stack([np.asarray(res.results[b]["out"]) for b in range(B)], axis=0).astype(np.float32)


class QPool:
    def __init__(self, P, scope, name, nbanks, dt=F32):
        self.items = []
        nq = 4 if dt == F32 else 8
        banks = [P.ps(scope, name, [128, 128 * nq], dt) for b in range(nbanks)]
        for q in range(nq):
            for t in banks:
                self.items.append((t[:, q * 128:(q + 1) * 128], t.b))
        self.i = 0

    def next(self):
        it = self.items[self.i]
        self.i = (self.i + 1) % len(self.items)
        return it


def phase_outproj(C, li, w_out, KB, yT, xin, xout):
    P = C.P
    NTOK, NT = C.NTOK, C.NT
    with ExitStack() as s:
        wsb = P.sb(s, "wo", [128, KB, 1024], BF16)
        gbc = [P.sb(s, "gbc", [128, 1024], F32) for _ in range(2)]
        ypool = P.pool(s, "yt", [128, KB, 512], BF16, 2)
        xpool = P.pool(s, "xo", [128, 1024], F32, 2)
        opool = P.pool(s, "oo", [128, 1024], F32, 2)
        tpool = P.pool(s, "ot", [128, 512], F32, 2)
        psp = P.pool(s, "pso", [128, 512], F32, 4, psum=True)
        for ch in range(2):
            c0 = ch * 1024
            for kb in range(0, KB, 4):
                P.dma("gpsimd", wsb[:, kb:kb + 4, :],
                      w_out[kb * 128:(kb + 4) * 128, c0:c0 + 1024].rearrange("(k p) c -> p k c", p=128),
                      writes=[wsb])
            for w in range(2):
                P.dma("sync", gbc[w][:], C.gate_row[li, w, c0:c0 + 1024].partition_broadcast(128),
                      writes=[gbc[w]])
            for t0 in range(0, NTOK, 512):
                tw = min(512, NTOK - t0)
                yt = ypool.next()
                P.dma("sync", yt[:, :, :tw], yT[:, t0:t0 + tw].rearrange("(k p) t -> p k t", p=128),
                      writes=[yt])
                for ts_ in range(0, tw, 128):
                    tok0 = t0 + ts_
                    w = 1 if tok0 < C.NCTX else 0
                    xt = xpool.next()
                    ot = opool.next()
                    P.dma("sync", xt[:], xin[tok0:tok0 + 128, c0:c0 + 1024], writes=[xt])
                    for cg in range(2):
                        ps = psp.next()
                        for k in range(KB):
                            P.mm(ps[:], yt[:, k, ts_:ts_ + 128], wsb[:, k, cg * 512:(cg + 1) * 512],
                                 k == 0, k == KB - 1, [yt, wsb], [ps])
                        tmp = tpool.next()
                        P.tt("vector", tmp[:], ps[:], gbc[w][:, cg * 512:(cg + 1) * 512], ALU.mult,
                             [ps, gbc[w]], [tmp])
                        P.tt("gpsimd", ot[:, cg * 512:(cg + 1) * 512], tmp[:], xt[:, cg * 512:(cg + 1) * 512],
                             ALU.add, [tmp, xt], [ot])
                    P.dma("sync", xout[tok0:tok0 + 128, c0:c0 + 1024], ot[:], reads=[ot])


def dn_layer(C, li, l, xin, xout):
    P, I, nc = C.P, C.I, C.nc
    NTOK, NT, NTC, NCTX, NLAT = C.NTOK, C.NT, C.NTC, C.NCTX, C.NLAT
    dbg = C.cfg.get("debug", [])
    w_in = I["w_in%d" % li]
    pre = C.dint("pre%d" % li, [8192, NTOK])
    zs = C.dint("zs%d" % li, [NTOK, 4096])
    ba = C.dint("ba%d" % li, [NTOK, 128])
    qT = C.dint("qT%d" % li, [16, 128, NTOK], BF16)
    kT = C.dint("kT%d" % li, [16, 128, NTOK], BF16)
    ktok = C.dint("ktok%d" % li, [16, NTOK, 128], BF16)
    vtok = C.dint("vtok%d" % li, [32, NTOK, 128], BF16)
    od = C.dint("od%d" % li, [2, 32, NTOK, 128])
    yT = C.dint("yT%d" % li, [4096, NTOK], BF16)
    stop = C.cfg.get("dn_stop", 99)

    with ExitStack() as s:
        C.hT = P.sb(s, "hT", [128, KD, NTOK], BF16)
        phase_norm(C, li, xin)
        P.barrier()
        if "hT" in dbg and li == 0:
            dump(C, "hT", C.hT, [128, KD, NTOK], BF16)
        with ExitStack() as s2:
            wfm = P.pool(s2, "wfm", [128, KD, 128], BF16, 3)
            stage = P.pool(s2, "stage", [128, NTOK], F32, 2)
            psp = P.pool(s2, "pp", [128, 512], F32, 4, psum=True)
            ev = 0
            for m in range(64):
                wt = wfm.next()
                P.dma("gpsimd", wt[:], w_in[:, m * 128:(m + 1) * 128].rearrange("(k p) c -> p k c", p=128),
                      writes=[wt])
                st = stage.next()
                for t0 in range(0, NTOK, 512):
                    tw = min(512, NTOK - t0)
                    ps = psp.next()
                    for k in range(KD):
                        P.mm(ps[:, :tw], wt[:, k, :], C.hT[:, k, t0:t0 + tw], k == 0, k == KD - 1,
                             [wt] + [C.hT.sub((tt, k)) for tt in range(t0 // 128, (t0 + tw) // 128)], [ps])
                    P.cp("scalar" if ev % 2 == 0 else "vector", st[:, t0:t0 + tw], ps[:, :tw], [ps], [st])
                    ev += 1
                P.dma("sync", pre[m * 128:(m + 1) * 128, :], st[:], reads=[st])
        P.barrier()
        with ExitStack() as s2:
            wtm = P.pool(s2, "wtm", [128, KD, 512], BF16, 2)
            zst = P.pool(s2, "zst", [128, 512], F32, 3)
            psp = P.pool(s2, "pp2", [128, 512], F32, 4, psum=True)
            for g in range(9):
                cw = 512 if g < 8 else 128
                c0 = 8192 + g * 512
                wt = wtm.next()
                P.dma("gpsimd", wt[:, :, :cw], w_in[:, c0:c0 + cw].rearrange("(k p) c -> p k c", p=128),
                      writes=[wt])
                for tt in range(NT):
                    ps = psp.next()
                    for k in range(KD):
                        P.mm(ps[:, :cw], C.hT[:, k, tt * 128:(tt + 1) * 128], wt[:, k, :cw], k == 0, k == KD - 1,
                             [wt, C.hT.sub((tt, k))], [ps])
                    zt = zst.next()
                    if g < 8:
                        P.act(zt[:, :cw], ps[:, :cw], AF.Silu, [ps], [zt])
                        P.dma("sync", zs[tt * 128:(tt + 1) * 128, g * 512:(g + 1) * 512], zt[:, :cw], reads=[zt])
                    else:
                        P.cp("vector", zt[:, :cw], ps[:, :cw], [ps], [zt])
                        P.dma("sync", ba[tt * 128:(tt + 1) * 128, :], zt[:, :cw], reads=[zt])
    P.barrier()
    if stop <= 1:
        return

    L = NTOK + 2
    with ExitStack() as s:
        cw_sb = P.sb(s, "cw", [128, 64, 5], F32)
        P.dma("sync", cw_sb[:], I["conv_w%d" % li].rearrange("(c p) j -> p c j", p=128), writes=[cw_sb])
        ppool = P.pool(s, "cpad", [128, NTOK + 6], F32, 2)
        for t in ppool.tiles:
            P.op("gpsimd", lambda e, t=t: e.memset(t[:], 0.0), [], [t])
        acc_p = P.pool(s, "cacc", [128, L], F32, 2)
        post_p = P.pool(s, "cpost", [128, NTOK], F32, 2)
        sq_p = P.pool(s, "csq", [128, NTOK], F32, 1)
        rn_p = P.pool(s, "crn", [128, 512], F32, 2)
        fbf_p = P.pool(s, "cfbf", [128, NTOK], BF16, 2)
        tk_p = P.pool(s, "ctk", [128, NT, 128], BF16, 2)
        ps_n = P.pool(s, "psn", [128, 512], F32, 2, psum=True)
        ps_t = P.pool(s, "pst", [128, 1024], BF16, 2, psum=True)
        ones = cmat(C, "ones")
        identb = cmat(C, "ident", bf=True)
        for cc in range(64):
            pb = ppool.next()
            P.dma("sync", pb[:, 2:2 + NCTX], pre[cc * 128:(cc + 1) * 128, 0:NCTX], writes=[pb])
            P.dma("sync", pb[:, NCTX + 4:NCTX + 4 + NLAT], pre[cc * 128:(cc + 1) * 128, NCTX:NTOK], writes=[pb])
            acc = acc_p.next()
            P.ts("vector", acc[:], pb[:, 0:L], cw_sb[:, cc, 0:1], None, ALU.mult, None, [pb, cw_sb], [acc])
            for j in range(1, 5):
                P.stt(acc[:], pb[:, j:j + L], cw_sb[:, cc, j:j + 1], acc[:], ALU.mult, ALU.add,
                      [pb, cw_sb, acc], [acc])
            fbf = fbf_p.next()
            if cc < 32:
                post = post_p.next()
                P.act(post[:, 0:NCTX], acc[:, 0:NCTX], AF.Silu, [acc], [post])
                P.act(post[:, NCTX:NTOK], acc[:, NCTX + 2:NCTX + 2 + NLAT], AF.Silu, [acc], [post])
                sq = sq_p.next()
                P.tt("gpsimd", sq[:], post[:], post[:], ALU.mult, [post], [sq])
                qscale = (128.0 ** -0.5) if cc < 16 else 1.0
                for t0 in range(0, NTOK, 512):
                    tw = min(512, NTOK - t0)
                    ps = ps_n.next()
                    P.mm(ps[:, :tw], ones, sq[:, t0:t0 + tw], True, True, [C.consts, sq], [ps])
                    rn = rn_p.next()
                    P.ts("vector", rn[:, :tw], ps[:, :tw], 1e-6, None, ALU.add, None, [ps], [rn])
                    P.act(rn[:, :tw], rn[:, :tw], AF.Sqrt, [rn], [rn])
                    P.op("vector", lambda e, rn=rn, tw=tw: e.reciprocal(out=rn[:, :tw], in_=rn[:, :tw]), [rn], [rn])
                    P.stt(fbf[:, t0:t0 + tw], post[:, t0:t0 + tw], qscale, rn[:, :tw], ALU.mult, ALU.mult,
                          [post, rn], [fbf])
                h = cc % 16
                P.dma("sync", (qT if cc < 16 else kT)[h, :, :], fbf[:], reads=[fbf])
            else:
                P.act(fbf[:, 0:NCTX], acc[:, 0:NCTX], AF.Silu, [acc], [fbf])
                P.act(fbf[:, NCTX:NTOK], acc[:, NCTX + 2:NCTX + 2 + NLAT], AF.Silu, [acc], [fbf])
            if cc >= 16:
                tk = tk_p.next()
                for t8 in range(0, NT, 8):
                    n8 = min(8, NT - t8)
                    ps = ps_t.next()
                    for j in range(n8):
                        tt = t8 + j
                        P.tr(ps[:, j * 128:(j + 1) * 128], fbf[:, tt * 128:(tt + 1) * 128], identb,
                             [fbf, C.cbf], [ps])
                    P.cp("scalar" if (t8 // 8) % 2 == 0 else "vector",
                         tk[:, t8:t8 + n8, :], ps[:, :n8 * 128].rearrange("p (n e) -> p n e", e=128), [ps], [tk])
                dst = ktok[cc - 16] if cc < 32 else vtok[cc - 32]
                P.dma("sync", dst.rearrange("(n p) e -> p n e", p=128), tk[:], reads=[tk])
    P.barrier()
    if stop <= 2:
        return
    dn_scan(C, li, ba, qT, kT, ktok, vtok, od)
    P.barrier()
    if stop <= 3:
        return
    dn_epilogue(C, li, od, zs, yT)
    P.barrier()
    if stop <= 4:
        return
    phase_outproj(C, li, I["w_out%d" % li], 32, yT, xin, xout)


def dn_unit(C, W, G, H, qa, qd, qb, hq, hv_i, d, n, S, Sbf, Gm, ATm, od):
    P = C.P
    sfx = "F" if d == 0 else "B"
    tri = cmat(C, "tri" + sfx)
    strc = cmat(C, "str" + sfx)
    ident = cmat(C, "ident")
    h = 2 * hq + hv_i
    col = d * 32 + h
    gcol = G.g_all[:, n, col:col + 1]
    beta = G.beta[:, n, col:col + 1]
    nbeta = G.nbeta[:, n, col:col + 1]
    egc = G.eg[:, n, d, h:h + 1]
    egr = G.eg[:, n, d, 32 + h:33 + h]
    egt = G.eg[:, n, d, 64 + h:65 + h]
    nsl = slice(n * 128, (n + 1) * 128)
    gS = W["gS"].next()
    P.act(gS[:], strc, AF.Identity, [C.consts, G.g_all], [gS], scale=gcol)
    yield
    pD, bD = qa.next()
    P.mm(pD, gS[:], tri, True, True, [gS, C.consts], [bD])
    dec = W["dec"].next()
    P.act(dec[:], pD, AF.Exp, [bD], [dec])
    yield
    X = W["Xb"].next()
    P.stt(X[:], Gm[:], beta, dec[:], ALU.mult, ALU.mult, [Gm, G.beta, dec], [X])
    AI = W["AI"].next()
    P.tt("gpsimd", AI[:], ATm[:], dec[:], ALU.mult, [ATm, dec], [AI])
    yield
    identb = cmat(C, "ident", bf=True)
    pL, bL = qb.next()
    P.tr(pL, X[:], identb, [X, C.cbf], [bL])
    Lf = W["Lfb"].next()
    P.cp("scalar", Lf[:], pL, [bL], [Lf])
    Xd = W["Xb"].next()
    P.tt("vector", Xd[:], X[:], cmat(C, "m0"), ALU.mult, [X, C.consts], [Xd])
    yield
    Ld = W["Lb"].next()
    P.tt("vector", Ld[:], Lf[:], cmat(C, "m0"), ALU.mult, [Lf, C.consts], [Ld])
    R = W["Rb"].next()
    P.tt("vector", R[:], ident, Xd[:], ALU.subtract, [C.consts, Xd], [R])
    yield
    pa, ba_ = qa.next()
    P.mm(pa, Ld[:], Xd[:], True, True, [Ld, Xd], [ba_])
    Xd2 = W["Xb"].next()
    P.cp("scalar", Xd2[:], pa, [ba_], [Xd2])
    pb, bb = qa.next()
    P.mm(pb, Xd[:], Ld[:], True, True, [Xd, Ld], [bb])
    Ld2 = W["Lb"].next()
    P.cp("scalar", Ld2[:], pb, [bb], [Ld2])
    yield
    pc, bc = qd.next()
    P.mm(pc, Ld2[:], R[:], True, True, [Ld2, R], [bc])
    R1 = W["Rb"].next()
    P.tt("vector", R1[:], pc, R[:], ALU.add, [bc, R], [R1])
    pd, bd = qa.next()
    P.mm(pd, Xd2[:], Ld2[:], True, True, [Xd2, Ld2], [bd])
    Ld4 = W["Lb"].next()
    P.cp("scalar", Ld4[:], pd, [bd], [Ld4])
    yield
    pe, be = qd.next()
    P.mm(pe, Ld4[:], R1[:], True, True, [Ld4, R1], [be])
    Dm = W["Rb"].next()
    P.tt("vector", Dm[:], pe, R1[:], ALU.add, [be, R1], [Dm])
    yield
    Tt = None
    for sz in (8, 16, 32, 64):
        pt, bt = qb.next()
        P.tr(pt, Dm[:], identb, [Dm, C.cbf], [bt])
        Dt = W["Dtb"].next()
        P.cp("scalar", Dt[:], pt, [bt], [Dt])
        pE, bE = qd.next()
        P.mm(pE, Lf[:], Dm[:], True, True, [Lf, Dm], [bE])
        E = W["Eb"].next()
        P.tt("vector", E[:], pE, cmat(C, "m%d" % sz), ALU.mult, [bE, C.consts], [E])
        yield
        pF, bF = qd.next()
        P.mm(pF, Dt[:], E[:], True, True, [Dt, E], [bF])
        if sz < 64:
            Dn = W["Rb"].next()
        else:
            Dn = Tt = W["Tt"].next()
        P.tt("vector", Dn[:], Dm[:], pF, ALU.subtract, [Dm, bF], [Dn])
        yield
        Dm = Dn
    kg = W["kg"].next()
    P.act(kg[:], H.ktn[:], AF.Identity, [H.ktn, G.eg], [kg], scale=egc)
    pu, bu = qa.next()
    P.mm(pu, Tt[:], H.vn[hv_i][:], True, True, [Tt, H.vn[hv_i]], [bu])
    u = W["u"].next()
    P.act(u[:], pu, AF.Identity, [bu, G.beta], [u], scale=beta)
    yield
    pw, bw = qa.next()
    P.mm(pw, kg[:], Tt[:], True, True, [kg, Tt], [bw])
    wT = W["wT"].next()
    P.cp("scalar", wT[:], pw, [bw], [wT])
    yield
    pp, bp = qd.next()
    P.mm(pp, wT[:], Sbf[:], True, True, [wT, Sbf], [bp])
    vn = W["vn"].next()
    P.stt(vn[:], pp, nbeta, u[:], ALU.mult, ALU.add, [bp, G.nbeta, u], [vn])
    yield
    po1, bo1 = qd.next()
    P.mm(po1, H.qTn[:], Sbf[:], True, True, [H.qTn, Sbf], [bo1])
    po2, bo2 = qa.next()
    P.mm(po2, AI[:], vn[:], True, True, [AI, vn], [bo2])
    o2 = W["o2"].next()
    P.cp("scalar", o2[:], po2, [bo2], [o2])
    o = W["o"].next()
    P.stt(o[:], po1, egc, o2[:], ALU.mult, ALU.add, [bo1, G.eg, o2], [o])
    P.dma("sync", od[d, h, n * 128:(n + 1) * 128, :], o[:], reads=[o])
    yield
    kd = W["kd"].next()
    P.act(kd[:], H.ktn[:], AF.Identity, [H.ktn, G.eg], [kd], scale=egr)
    pS, bS = qd.next()
    P.mm(pS, kd[:], vn[:], True, True, [kd, vn], [bS])
    P.stt(S[:], S[:], egt, pS, ALU.mult, ALU.add, [S, G.eg, bS], [S])
    P.cp("scalar", Sbf[:], S[:], [S], [Sbf])
    yield


def dn_scan(C, li, ba, qT, kT, ktok, vtok, od):
    P, I = C.P, C.I
    NT, NTC, NTOK = C.NT, C.NTC, C.NTOK
    nheads = C.cfg.get("dn_heads", 16)
    with ExitStack() as s:
        G = Ctx()
        alog = P.sb(s, "alog", [128, 64], F32)
        dtb = P.sb(s, "dtb", [128, 64], F32)
        P.dma("sync", alog[:], I["a_log%d" % li].partition_broadcast(128), writes=[alog])
        P.dma("sync", dtb[:], I["dt_bias%d" % li].partition_broadcast(128), writes=[dtb])
        negA = P.sb(s, "negA", [128, 64], F32)
        P.act(negA[:], alog[:], AF.Exp, [alog], [negA])
        P.ts("vector", negA[:], negA[:], -1.0, None, ALU.mult, None, [negA], [negA])
        G.g_all = P.sb(s, "g_all", [128, NT, 64], F32)
        G.beta = P.sb(s, "beta", [128, NT, 64], F32)
        G.nbeta = P.sb(s, "nbeta", [128, NT, 64], F32)
        G.eg = P.sb(s, "eg", [128, NT, 2, 96], F32)
        gsc = ExitStack()
        ba_sb = P.sb(gsc, "ba", [128, NT, 128], F32)
        P.dma("sync", ba_sb[:], ba.rearrange("(n p) c -> p n c", p=128), writes=[ba_sb])
        xg = P.sb(gsc, "xg", [128, NT, 64], F32)
        t1 = P.sb(gsc, "gt1", [128, NT, 64], F32)
        t2 = P.sb(gsc, "gt2", [128, NT, 64], F32)
        for d in range(2):
            P.act(G.beta[:, :, d * 32:(d + 1) * 32], ba_sb[:, :, d * 64:d * 64 + 32], AF.Sigmoid,
                  [ba_sb], [G.beta])
            for n in range(NT):
                P.tt("gpsimd", xg[:, n, d * 32:(d + 1) * 32], ba_sb[:, n, d * 64 + 32:d * 64 + 64],
                     dtb[:, d * 32:(d + 1) * 32], ALU.add, [ba_sb, dtb], [xg])
        P.ts("gpsimd", G.nbeta[:], G.beta[:], -1.0, None, ALU.mult, None, [G.beta], [G.nbeta])
        P.ts("gpsimd", t2[:], xg[:], -1.0, None, ALU.mult, None, [xg], [t2])
        P.tt("vector", t1[:], xg[:], t2[:], ALU.max, [xg, t2], [t1])
        P.act(t1[:], t1[:], AF.Exp, [t1], [t1], scale=-1.0)
        P.ts("vector", t1[:], t1[:], 1.0, None, ALU.add, None, [t1], [t1])
        P.act(t1[:], t1[:], AF.Ln, [t1], [t1])
        P.stt(t2[:], xg[:], 0.0, t1[:], ALU.max, ALU.add, [xg, t1], [t2])
        for n in range(NT):
            P.tt("gpsimd", G.g_all[:, n, :], t2[:, n, :], negA[:], ALU.mult, [t2, negA], [G.g_all])
        P.barrier()
        gsc.close()
        qa = QPool(P, s, "qpa", 3)
        qd = QPool(P, s, "qpd", 3)
        qb = QPool(P, s, "qpb", 2, dt=BF16)
        for n in range(NT):
            for d in range(2):
                sfx = "F" if d == 0 else "B"
                pg, bg = qa.next()
                rhs = G.g_all[:, n, d * 32:(d + 1) * 32]
                P.mm(pg[:, 0:32], cmat(C, "tri" + sfx), rhs, True, True, [C.consts, G.g_all], [bg])
                P.mm(pg[:, 32:64], cmat(C, "ntri" + sfx), rhs, True, True, [C.consts, G.g_all], [bg])
                P.mm(pg[:, 64:96], cmat(C, "ones"), rhs, True, True, [C.consts, G.g_all], [bg])
                P.act(G.eg[:, n, d, :], pg[:, 0:96], AF.Exp, [bg], [G.eg])
        if "gates" in C.cfg.get("debug", []):
            dump(C, "g_all", G.g_all, [128, NT, 64], F32)
            dump(C, "beta", G.beta, [128, NT, 64], F32)
            dump(C, "eg", G.eg, [128, NT, 2, 96], F32)
        W = {}
        for name, dt, n in [("gS", F32, 12), ("dec", F32, 16), ("Xb", BF16, 32), ("Lb", BF16, 32), ("Rb", BF16, 40),
                            ("u", F32, 16), ("o2", F32, 12), ("o", F32, 12), ("Gm", F32, 8), ("ATm", F32, 8),
                            ("AI", BF16, 16), ("Tt", BF16, 16), ("Lfb", BF16, 12), ("Dtb", BF16, 16), ("Eb", BF16, 16),
                            ("kg", BF16, 16), ("wT", BF16, 16), ("vn", BF16, 16), ("kd", BF16, 16)]:
            W[name] = P.pool(s, "w" + name, [128, 128], dt, n)
        NCH = 12
        kTp = P.pool(s, "kTn", [128, 128], BF16, NCH)
        qTp = P.pool(s, "qTn", [128, 128], BF16, NCH)
        ktp = P.pool(s, "ktn", [128, 128], BF16, NCH)
        vtp = P.pool(s, "vtn", [128, 128], BF16, 2 * NCH)
        HP = 2
        Ss = [P.sb(s, "S", [128, 128], F32) for _ in range(4 * HP)]
        Sb = [P.sb(s, "Sb", [128, 128], BF16) for _ in range(4 * HP)]
        ordF = list(range(NT))
        ordB = list(range(NTC - 1, -1, -1)) + list(range(NT - 1, NTC - 1, -1))
        for hp in range(0, nheads, HP):
            for t_ in Ss + Sb:
                P.op("gpsimd", lambda e, t_=t_: e.memset(t_[:], 0.0), [], [t_])
            for t in range(NT):
                gens = []
                for hi in range(HP):
                    hq = hp + hi
                    for d in range(2):
                        n = (ordF if d == 0 else ordB)[t]
                        sfx = "F" if d == 0 else "B"
                        nsl = slice(n * 128, (n + 1) * 128)
                        H = Ctx()
                        H.kTn, H.qTn, H.ktn = kTp.next(), qTp.next(), ktp.next()
                        H.vn = [vtp.next(), vtp.next()]
                        P.dma("sync", H.kTn[:], kT[hq, :, nsl], writes=[H.kTn])
                        P.dma("sync", H.qTn[:], qT[hq, :, nsl], writes=[H.qTn])
                        P.dma("sync", H.ktn[:], ktok[hq, nsl, :], writes=[H.ktn])
                        for i in range(2):
                            P.dma("sync", H.vn[i][:], vtok[2 * hq + i, nsl, :], writes=[H.vn[i]])
                        pg, bg = qd.next()
                        P.mm(pg, H.kTn[:], H.kTn[:], True, True, [H.kTn], [bg])
                        Gm = W["Gm"].next()
                        P.tt("vector", Gm[:], pg, cmat(C, "mts" + sfx), ALU.mult, [bg, C.consts], [Gm])
                        pa, ba_ = qd.next()
                        P.mm(pa, H.kTn[:], H.qTn[:], True, True, [H.kTn, H.qTn], [ba_])
                        ATm = W["ATm"].next()
                        P.tt("vector", ATm[:], pa, cmat(C, "mti" + sfx), ALU.mult, [ba_, C.consts], [ATm])
                        for hv_i in range(2):
                            ci = hi * 4 + d * 2 + hv_i
                            gens.append(dn_unit(C, W, G, H, qa, qd, qb, hq, hv_i, d, n, Ss[ci], Sb[ci], Gm, ATm, od))
                live = gens
                while live:
                    nxt = []
                    for g in live:
                        try:
                            next(g)
                            nxt.append(g)
                        except StopIteration:
                            pass
                    live = nxt


def dn_epilogue(C, li, od, zs, yT):
    P, I = C.P, C.I
    NT, NTOK = C.NT, C.NTOK
    nhv = 2 * C.cfg.get("dn_heads", 16)
    with ExitStack() as s:
        o0p = P.pool(s, "eo0", [128, NT, 128], F32, 2)
        o1p = P.pool(s, "eo1", [128, NT, 128], F32, 2)
        zp = P.pool(s, "ez", [128, NT, 128], F32, 2)
        ybp = P.pool(s, "eyb", [128, NT, 128], BF16, 2)
        ysp = P.pool(s, "eys", [128, NTOK], BF16, 2)
        msp = P.pool(s, "ems", [128, NT], F32, 2)
        onw = P.sb(s, "onw", [128, 128], F32)
        P.dma("sync", onw[:], I["onorm_w%d" % li].partition_broadcast(128), writes=[onw])
        ps_t = P.pool(s, "pse", [128, 1024], BF16, 2, psum=True)
        identb = cmat(C, "ident", bf=True)
        for hv in range(nhv):
            o0, o1, z, yb, ys, ms = o0p.next(), o1p.next(), zp.next(), ybp.next(), ysp.next(), msp.next()
            P.dma("sync", o0[:], od[0, hv].rearrange("(n p) e -> p n e", p=128), writes=[o0])
            P.dma("sync", o1[:], od[1, hv].rearrange("(n p) e -> p n e", p=128), writes=[o1])
            P.dma("sync", z[:], zs[:, hv * 128:(hv + 1) * 128].rearrange("(n p) e -> p n e", p=128), writes=[z])
            P.tt("vector", o0[:], o0[:], o1[:], ALU.add, [o0, o1], [o0])
            P.tt("vector", o1[:], o0[:], o0[:], ALU.mult, [o0], [o1])
            P.op("vector", lambda e, ms=ms, o1=o1: e.tensor_reduce(out=ms[:], in_=o1[:], axis=AX.X, op=ALU.add),
                 [o1], [ms])
            P.ts("vector", ms[:], ms[:], 1.0 / 128, 1e-6, ALU.mult, ALU.add, [ms], [ms])
            P.act(ms[:], ms[:], AF.Sqrt, [ms], [ms])
            P.op("vector", lambda e, ms=ms: e.reciprocal(out=ms[:], in_=ms[:]), [ms], [ms])
            for n in range(NT):
                P.tt("vector" if n % 3 else "gpsimd", z[:, n, :], z[:, n, :], onw[:], ALU.mult, [z, onw], [z])
            for n in range(NT):
                P.stt(yb[:, n, :], o0[:, n, :], ms[:, n:n + 1], z[:, n, :], ALU.mult, ALU.mult, [o0, ms, z], [yb])
            for t8 in range(0, NT, 8):
                n8 = min(8, NT - t8)
                ps = ps_t.next()
                for j in range(n8):
                    P.tr(ps[:, j * 128:(j + 1) * 128], yb[:, t8 + j, :], identb, [yb, C.cbf], [ps])
                P.cp("scalar" if (t8 // 8) % 2 == 0 else "vector", ys[:, t8 * 128:(t8 + n8) * 128],
                     ps[:, :n8 * 128], [ps], [ys])
            P.dma("sync", yT[hv * 128:(hv + 1) * 128, :], ys[:], reads=[ys])


def proj_tokmajor(C, w_in, c0, ncols, func, dst, dst_c0, scope_name, out_dt=F32):
    P = C.P
    with ExitStack() as s2:
        wtm = P.pool(s2, scope_name + "w", [128, KD, 512], BF16, 2)
        zst = P.pool(s2, scope_name + "z", [128, 512], out_dt, 3)
        psp = P.pool(s2, scope_name + "p", [128, 512], F32, 4, psum=True)
        for g in range(0, ncols, 512):
            cw = min(512, ncols - g)
            wt = wtm.next()
            P.dma("gpsimd", wt[:, :, :cw], w_in[:, c0 + g:c0 + g + cw].rearrange("(k p) c -> p k c", p=128),
                  writes=[wt])
            for tt in range(C.NT):
                ps = psp.next()
                for k in range(KD):
                    P.mm(ps[:, :cw], C.hT[:, k, tt * 128:(tt + 1) * 128], wt[:, k, :cw], k == 0, k == KD - 1,
                         [wt, C.hT.sub((tt, k))], [ps])
                zt = zst.next()
                if func is None:
                    P.cp("vector", zt[:, :cw], ps[:, :cw], [ps], [zt])
                else:
                    P.act(zt[:, :cw], ps[:, :cw], func, [ps], [zt])
                P.dma("sync", dst[tt * 128:(tt + 1) * 128, dst_c0 + g:dst_c0 + g + cw], zt[:, :cw], reads=[zt])


def tok_tiles(C):
    tiles = []
    for t0 in range(0, C.NCTX, 512):
        tiles.append((t0, min(512, C.NCTX - t0), True))
    for t0 in range(C.NCTX, C.NTOK, 512):
        tiles.append((t0, min(512, C.NTOK - t0), False))
    return tiles


def da_layer(C, li, l, xin, xout):
    import math
    P, I = C.P, C.I
    NTOK, NT, NTC, NCTX, NLAT = C.NTOK, C.NT, C.NTC, C.NCTX, C.NLAT
    w_in = I["w_in%d" % li]
    qT = C.dint("qT%d" % li, [16, 128, NTOK], BF16)
    kT = C.dint("kT%d" % li, [16, 128, NTOK], BF16)
    vtok = C.dint("vtok%d" % li, [NTOK, 2048], BF16)
    gs = C.dint("gs%d" % li, [NTOK, 2048])
    yT = C.dint("yT%d" % li, [2048, NTOK], BF16)
    lambda_init = 0.8 - 0.6 * math.exp(-0.3 * l)
    stop = C.cfg.get("da_stop", 99)
    rope = I["rope%d" % li]
    with ExitStack() as s:
        C.hT = P.sb(s, "hT", [128, KD, NTOK], BF16)
        phase_norm(C, li, xin)
        P.barrier()
        with ExitStack() as s2:
            wfm = P.pool(s2, "wfm", [128, KD, 128], BF16, 3)
            stage = P.pool(s2, "stage", [128, NTOK], BF16, 2)
            csp = P.pool(s2, "cs", [128, 2, 512], F32, 2)
            qbp = P.pool(s2, "qb", [128, 512], BF16, 2)
            t1p = P.pool(s2, "t1", [128, 512], F32, 2)
            t2p = P.pool(s2, "t2", [128, 512], F32, 2)
            psp = P.pool(s2, "pp", [128, 512], F32, 3, psum=True)
            psr = P.pool(s2, "pr", [128, 512], F32, 2, psum=True)
            rotm = cmat(C, "rotm", bf=True)
            for m in range(32):
                wt = wfm.next()
                P.dma("gpsimd", wt[:], w_in[:, m * 128:(m + 1) * 128].rearrange("(k p) c -> p k c", p=128),
                      writes=[wt])
                st = stage.next()
                for (t0, tw, isctx) in tok_tiles(C):
                    ps = psp.next()
                    for k in range(KD):
                        P.mm(ps[:, :tw], wt[:, k, :], C.hT[:, k, t0:t0 + tw], k == 0, k == KD - 1,
                             [wt] + [C.hT.sub((tt, k)) for tt in range(t0 // 128, (t0 + tw) // 128)], [ps])
                    if isctx or "norope" in C.cfg.get("debug", []):
                        P.cp("scalar", st[:, t0:t0 + tw], ps[:, :tw], [ps], [st])
                        continue
                    cs = csp.next()
                    if "r2" not in C.cfg.get("debug", []):
                        P.dma("sync", cs[:, :, :tw], rope[:, :, t0 - NCTX:t0 - NCTX + tw], writes=[cs])
                    qb = qbp.next()
                    P.cp("vector", qb[:, :tw], ps[:, :tw], [ps], [qb])
                    dbgf = C.cfg.get("debug", [])
                    if "r1" in dbgf:
                        pr = ps
                    else:
                        pr = psr.next()
                        P.mm(pr[:, :tw], rotm, qb[:, :tw], True, True, [C.cbf, qb], [pr])
                    t1 = t1p.next()
                    P.tt("vector", t1[:, :tw], ps[:, :tw], cs[:, 0, :tw], ALU.mult, [ps, cs], [t1])
                    t2 = t2p.next()
                    P.tt("vector", t2[:, :tw], pr[:, :tw], cs[:, 1, :tw], ALU.mult, [pr, cs], [t2])
                    P.tt("gpsimd", st[:, t0:t0 + tw], t1[:, :tw], t2[:, :tw], ALU.add, [t1, t2], [st])
                P.dma("sync", (qT if m < 16 else kT)[m % 16, :, :], st[:], reads=[st])
        P.barrier()
        if "nov" not in C.cfg.get("debug", []):
            proj_tokmajor(C, w_in, 4096, 2048, None, vtok, 0, "dav", out_dt=BF16)
        P.barrier()
        if "nog" not in C.cfg.get("debug", []):
            proj_tokmajor(C, w_in, 6144, 2048, AF.Silu, gs, 0, "dag")
    P.barrier()
    if stop <= 1:
        return

    scale = 128.0 ** -0.5
    with ExitStack() as s:
        lam = P.sb(s, "lam", [128, 4, 128], F32)
        P.dma("sync", lam[:], I["lam%d" % li].partition_broadcast(128), writes=[lam])
        lt = P.sb(s, "lamt", [128, 128], F32)
        lsc = P.sb(s, "lamsc", [128, 4], F32)
        for i in range(2):
            P.tt("vector", lt[:], lam[:, 2 * i, :], lam[:, 2 * i + 1, :], ALU.mult, [lam], [lt])
            P.op("vector", lambda e, i=i: e.tensor_reduce(out=lsc[:, i:i + 1], in_=lt[:], axis=AX.X, op=ALU.add),
                 [lt], [lsc])
        P.act(lsc[:, 0:2], lsc[:, 0:2], AF.Exp, [lsc], [lsc])
        P.tt("vector", lsc[:, 2:3], lsc[:, 1:2], lsc[:, 0:1], ALU.subtract, [lsc], [lsc])
        P.ts("vector", lsc[:, 3:4], lsc[:, 2:3], -lambda_init, None, ALU.add, None, [lsc], [lsc])
        neglam = lsc[:, 3:4]
        subw = P.sb(s, "subw", [128, 256], F32)
        P.dma("sync", subw[:], I["subln%d" % li].partition_broadcast(128), writes=[subw])
        P.ts("vector", subw[:], subw[:], 1.0 - lambda_init, None, ALU.mult, None, [subw], [subw])
        qp_ = P.pool(s, "aq", [128, NTOK], BF16, 4)
        kp_ = P.pool(s, "ak", [128, NTOK], BF16, 4)
        vp_ = P.pool(s, "av", [128, NT, 257], BF16, 2)
        for t in vp_.tiles:
            P.op("gpsimd", lambda e, t=t: e.memset(t[:, :, 256:257], 1.0), [], [t])
        ptp = P.pool(s, "apt", [128, 512], BF16, 3)
        ojp = P.pool(s, "aoj", [128, 256], F32, 4)
        op_ = P.pool(s, "ao", [128, 256], F32, 2)
        gp_ = P.pool(s, "ag", [128, 256], F32, 2)
        ybp = P.pool(s, "ayb", [128, 256], BF16, 2)
        ysp = P.pool(s, "ays", [128, 2, NTOK], BF16, 2)
        stp = P.pool(s, "ast", [128, 4], F32, 4)
        sqj = P.sb(s, "asq", [128, 256], BF16)
        pss = P.pool(s, "aps", [128, 512], F32, 2, psum=True)
        psa = [P.ps(s, "apa", [128, 512], F32) for _ in range(4)]
        pst = P.pool(s, "apt2", [128, 1024], BF16, 2, psum=True)
        identb = cmat(C, "ident", bf=True)
        for h in range(C.cfg.get("da_heads", 8)):
            qh = [qp_.next(), qp_.next()]
            kh = [kp_.next(), kp_.next()]
            vh = vp_.next()
            ys = ysp.next()
            for j in range(2):
                P.dma("sync", qh[j][:], qT[2 * h + j, :, :], writes=[qh[j]])
                P.dma("sync", kh[j][:], kT[2 * h + j, :, :], writes=[kh[j]])
            P.dma("sync", vh[:, :, 0:256], vtok[:, h * 256:(h + 1) * 256].rearrange("(n p) e -> p n e", p=128),
                  writes=[vh])
            qtiles = [(0, NCTX, 0, NTC)] + [(q0, min(512, NTOK - q0), 0, NT) for q0 in range(NCTX, NTOK, 512)]
            for (q0, qw, kb0, kb1) in qtiles:
                nsub = qw // 128
                ojs = []
                for j in range(2):
                    def score(kb, j=j, q0=q0, qw=qw):
                        ps = pss.next()
                        P.mm(ps[:, :qw], kh[j][:, kb * 128:(kb + 1) * 128], qh[j][:, q0:q0 + qw], True, True,
                             [kh[j], qh[j]], [ps])
                        return ps
                    ps_next = score(kb0)
                    for kb in range(kb0, kb1):
                        ps = ps_next
                        if kb + 1 < kb1:
                            ps_next = score(kb + 1)
                        pt = ptp.next()
                        P.act(pt[:, :qw], ps[:, :qw], AF.Exp, [ps], [pt], scale=scale)
                        for qs in range(nsub):
                            P.mm(psa[qs][:, 0:257], pt[:, qs * 128:(qs + 1) * 128], vh[:, kb, :], kb == kb0,
                                 kb == kb1 - 1, [pt, vh], [psa[qs]])
                    row = []
                    for qs in range(nsub):
                        st = stp.next()
                        P.op("vector", lambda e, st=st, qs=qs: e.reciprocal(out=st[:, 0:1], in_=psa[qs][:, 256:257]),
                             [psa[qs]], [st])
                        oj = ojp.next() if j == 0 else None
                        if j == 0:
                            P.ts("vector", oj[:], psa[qs][:, 0:256], st[:, 0:1], None, ALU.mult, None,
                                 [psa[qs], st], [oj])
                            row.append(oj)
                        else:
                            o1 = gp_.next()
                            P.ts("vector", o1[:], psa[qs][:, 0:256], st[:, 0:1], None, ALU.mult, None,
                                 [psa[qs], st], [o1])
                            o = op_.next()
                            P.stt(o[:], o1[:], neglam, ojs[qs][:], ALU.mult, ALU.add, [o1, lsc, ojs[qs]], [o])
                            P.act(sqj[:], o[:], AF.Square, [o], [sqj, st], accum_out=st[:, 1:2])
                            P.ts("vector", st[:, 2:3], st[:, 1:2], 1.0 / 256, 1e-5, ALU.mult, ALU.add, [st], [st])
                            P.act(st[:, 2:3], st[:, 2:3], AF.Sqrt, [st], [st])
                            P.op("vector", lambda e, st=st: e.reciprocal(out=st[:, 3:4], in_=st[:, 2:3]), [st], [st])
                            g = gp_.next()
                            tok0 = q0 + qs * 128
                            P.dma("sync", g[:], gs[tok0:tok0 + 128, h * 256:(h + 1) * 256], writes=[g])
                            P.tt("gpsimd", g[:], g[:], subw[:], ALU.mult, [g, subw], [g])
                            yb = ybp.next()
                            P.stt(yb[:], o[:], st[:, 3:4], g[:], ALU.mult, ALU.mult, [o, st, g], [yb])
                            pT = pst.next()
                            for c in range(2):
                                P.tr(pT[:, c * 128:(c + 1) * 128], yb[:, c * 128:(c + 1) * 128], identb,
                                     [yb, C.cbf], [pT])
                            P.cp("scalar", ys[:, :, tok0:tok0 + 128],
                                 pT[:, 0:256].rearrange("p (c t) -> p c t", c=2), [pT], [ys])
                    if j == 0:
                        ojs = row
            for c in range(2):
                P.dma("sync", yT[h * 256 + c * 128:h * 256 + (c + 1) * 128, :], ys[:, c, :], reads=[ys])
    P.barrier()
    if stop <= 2:
        return
    phase_outproj(C, li, I["w_out%d" % li], 16, yT, xin, xout)


def na_layer(C, li, l, xin, xout):
    P, I = C.P, C.I
    NTOK, NT, NTC, NCTX, NLAT = C.NTOK, C.NT, C.NTC, C.NCTX, C.NLAT
    w_in = I["w_in%d" % li]
    nab = I["nab%d" % li]
    qT = C.dint("qT%d" % li, [16, 128, NTOK], BF16)
    kT = C.dint("kT%d" % li, [16, 128, NTOK], BF16)
    vtok = C.dint("vtok%d" % li, [NTOK, 2048], BF16)
    gs = C.dint("gs%d" % li, [NTOK, 2048])
    od = C.dint("od%d" % li, [16, NTOK, 128])
    yT = C.dint("yT%d" % li, [2048, NTOK], BF16)
    stop = C.cfg.get("na_stop", 99)
    qscale = 128.0 ** -0.5
    with ExitStack() as s:
        C.hT = P.sb(s, "hT", [128, KD, NTOK], BF16)
        phase_norm(C, li, xin)
        P.barrier()
        with ExitStack() as s2:
            wfm = P.pool(s2, "wfm", [128, KD, 128], BF16, 3)
            stage = P.pool(s2, "stage", [128, NTOK], BF16, 2)
            psp = P.pool(s2, "pp", [128, 512], F32, 4, psum=True)
            ev = 0
            for m in range(32):
                wt = wfm.next()
                P.dma("gpsimd", wt[:], w_in[:, m * 128:(m + 1) * 128].rearrange("(k p) c -> p k c", p=128),
                      writes=[wt])
                st = stage.next()
                for t0 in range(0, NTOK, 512):
                    tw = min(512, NTOK - t0)
                    ps = psp.next()
                    for k in range(KD):
                        P.mm(ps[:, :tw], wt[:, k, :], C.hT[:, k, t0:t0 + tw], k == 0, k == KD - 1,
                             [wt] + [C.hT.sub((tt, k)) for tt in range(t0 // 128, (t0 + tw) // 128)], [ps])
                    sc = qscale if m < 16 else 1.0
                    if ev % 2 == 0:
                        P.act(st[:, t0:t0 + tw], ps[:, :tw], AF.Identity, [ps], [st], scale=sc)
                    else:
                        P.ts("vector", st[:, t0:t0 + tw], ps[:, :tw], sc, None, ALU.mult, None, [ps], [st])
                    ev += 1
                P.dma("sync", (qT if m < 16 else kT)[m % 16, :, :], st[:], reads=[st])
        P.barrier()
        proj_tokmajor(C, w_in, 4096, 2048, None, vtok, 0, "nav", out_dt=BF16)
        P.barrier()
        proj_tokmajor(C, w_in, 6144, 2048, AF.Silu, gs, 0, "nag")
    P.barrier()
    if stop <= 1:
        return
    rows = NLAT // 64
    wr = min(8, rows)
    with ExitStack() as s:
        qp_ = P.pool(s, "nq", [128, NTOK], BF16, 2)
        kp_ = P.pool(s, "nk", [128, NTOK], BF16, 2)
        vp_ = P.pool(s, "nv", [128, NT, 129], BF16, 2)
        for t in vp_.tiles:
            P.op("gpsimd", lambda e, t=t: e.memset(t[:, :, 128:129], 1.0), [], [t])
        nbp = P.pool(s, "nnb", [128, 19, 64], F32, 2)
        sbp = P.pool(s, "nsb", [128, 5, 64], F32, 3)
        ptp = P.pool(s, "npt", [128, 5 * 64 + NTC * 64], BF16, 3)
        ptc = P.pool(s, "nptc", [128, NCTX], BF16, 2)
        oap = P.pool(s, "noa", [64, rows, 128], F32, 2)
        ocp = P.pool(s, "noc", [128, NTC, 128], F32, 2)
        stp = P.pool(s, "nst", [128, 2], F32, 4)
        psw = P.pool(s, "npw", [128, 512], F32, 2, psum=True)
        psc = P.pool(s, "npc", [128, 512], F32, 2, psum=True)
        psa = P.pool(s, "npa", [128, 512], F32, 2, psum=True)
        for h in range(C.cfg.get("na_heads", 16)):
            qh, kh, vh, nb = qp_.next(), kp_.next(), vp_.next(), nbp.next()
            P.dma("sync", qh[:], qT[h, :, :], writes=[qh])
            P.dma("sync", kh[:], kT[h, :, :], writes=[kh])
            P.dma("sync", vh[:, :, 0:128], vtok[:, h * 128:(h + 1) * 128].rearrange("(n p) e -> p n e", p=128),
                  writes=[vh])
            P.dma("sync", nb[:], nab[h], writes=[nb])
            oa = oap.next()
            def stage1(r):
                rs = min(max(r - 4, 0), rows - wr)
                if rs % 2 == 0:
                    npair = wr // 2
                    pairs = [rs // 2 + i for i in range(npair)]
                    dr0 = rs - r + 7
                    btiles = [dr0 + 2 * i for i in range(npair)]
                else:
                    npair = 5
                    pairs = [(rs - 1) // 2 + i for i in range(5)]
                    btiles = [14 + i for i in range(5)]
                q_sl = qh[:, NCTX + r * 64:NCTX + (r + 1) * 64]
                pw = psw.next()
                for i, m in enumerate(pairs):
                    tt = NTC + m
                    P.mm(pw[:, i * 64:(i + 1) * 64], kh[:, tt * 128:(tt + 1) * 128], q_sl, True, True,
                         [kh, qh], [pw])
                pc = psc.next()
                for c in range(NTC):
                    P.mm(pc[:, c * 64:(c + 1) * 64], kh[:, c * 128:(c + 1) * 128], q_sl, True, True,
                         [kh, qh], [pc])
                sb = sbp.next()
                if btiles[-1] - btiles[0] == npair - 1:
                    P.tt("vector", sb[:, 0:npair, :], pw[:, 0:npair * 64].rearrange("p (i q) -> p i q", q=64),
                         nb[:, btiles[0]:btiles[0] + npair, :], ALU.add, [pw, nb], [sb])
                else:
                    for i in range(npair):
                        P.tt("vector", sb[:, i, :], pw[:, i * 64:(i + 1) * 64], nb[:, btiles[i], :], ALU.add,
                             [pw, nb], [sb])
                pt = ptp.next()
                P.act(pt[:, 0:npair * 64], sb[:, 0:npair, :], AF.Exp, [sb], [pt])
                P.act(pt[:, npair * 64:(npair + NTC) * 64], pc[:, 0:NTC * 64], AF.Exp, [pc], [pt])
                return (r, pt, pairs, npair)

            def stage2(r, pt, pairs, npair):
                acc = psa.next()
                for i, m in enumerate(pairs):
                    P.mm(acc[0:64, 0:129], pt[:, i * 64:(i + 1) * 64], vh[:, NTC + m, :], i == 0, False,
                         [pt, vh], [acc])
                for c in range(NTC):
                    P.mm(acc[0:64, 0:129], pt[:, (npair + c) * 64:(npair + c + 1) * 64], vh[:, c, :], False,
                         c == NTC - 1, [pt, vh], [acc])
                st = stp.next()
                P.op("vector", lambda e, st=st, acc=acc: e.reciprocal(out=st[0:64, 0:1], in_=acc[0:64, 128:129]),
                     [acc], [st])
                P.ts("vector", oa[0:64, r, :], acc[0:64, 0:128], st[0:64, 0:1], None, ALU.mult, None,
                     [acc, st], [oa])

            prev = None
            for r in range(rows):
                cur = stage1(r)
                if prev is not None:
                    stage2(*prev)
                prev = cur
            stage2(*prev)
            P.dma("sync", od[h, NCTX:NTOK, :].rearrange("(r p) e -> p r e", p=64), oa[:], reads=[oa])
            oc = ocp.next()
            accs = [psa.next() for _ in range(NTC)]
            for kb in range(NTC):
                pc = psc.next()
                P.mm(pc[:, 0:NCTX], kh[:, kb * 128:(kb + 1) * 128], qh[:, 0:NCTX], True, True, [kh, qh], [pc])
                pt2 = ptc.next()
                P.act(pt2[:, 0:NCTX], pc[:, 0:NCTX], AF.Exp, [pc], [pt2])
                for qs in range(NTC):
                    P.mm(accs[qs][:, 0:129], pt2[:, qs * 128:(qs + 1) * 128], vh[:, kb, :], kb == 0, kb == NTC - 1,
                         [pt2, vh], [accs[qs]])
            for qs in range(NTC):
                st = stp.next()
                P.op("vector", lambda e, st=st, a=accs[qs]: e.reciprocal(out=st[:, 0:1], in_=a[:, 128:129]),
                     [accs[qs]], [st])
                P.ts("vector", oc[:, qs, :], accs[qs][:, 0:128], st[:, 0:1], None, ALU.mult, None,
                     [accs[qs], st], [oc])
            P.dma("sync", od[h, 0:NCTX, :].rearrange("(n p) e -> p n e", p=128), oc[:], reads=[oc])
    P.barrier()
    if stop <= 2:
        return
    with ExitStack() as s:
        op_ = P.pool(s, "eo", [128, NT, 128], F32, 2)
        gp_ = P.pool(s, "eg", [128, NT, 128], F32, 2)
        ybp = P.pool(s, "eyb", [128, NT, 128], BF16, 2)
        ysp = P.pool(s, "eys", [128, NTOK], BF16, 2)
        ps_t = P.pool(s, "pse", [128, 1024], BF16, 2, psum=True)
        identb = cmat(C, "ident", bf=True)
        for h in range(16):
            o, g, yb, ys = op_.next(), gp_.next(), ybp.next(), ysp.next()
            P.dma("sync", o[:], od[h].rearrange("(n p) e -> p n e", p=128), writes=[o])
            P.dma("sync", g[:], gs[:, h * 128:(h + 1) * 128].rearrange("(n p) e -> p n e", p=128), writes=[g])
            P.tt("gpsimd" if h % 2 == 0 else "vector", yb[:], o[:], g[:], ALU.mult, [o, g], [yb])
            for t8 in range(0, NT, 8):
                n8 = min(8, NT - t8)
                ps = ps_t.next()
                for j in range(n8):
                    P.tr(ps[:, j * 128:(j + 1) * 128], yb[:, t8 + j, :], identb, [yb, C.cbf], [ps])
                P.cp("scalar" if (t8 // 8) % 2 == 0 else "vector", ys[:, t8 * 128:(t8 + n8) * 128],
                     ps[:, :n8 * 128], [ps], [ys])
            P.dma("sync", yT[h * 128:(h + 1) * 128, :], ys[:], reads=[ys])
    P.barrier()
    if stop <= 3:
        return
    phase_outproj(C, li, I["w_out%d" % li], 16, yT, xin, xout)
```
